# Optimizing a Trainium2 kernel written in Bass

```python
import math
import jax, jax.numpy as jnp
from jax import lax
import numpy as np

D_MODEL = 1024
BATCH = 32
SEQ = 2048
DEPTH = 2

GRID_W = 64
CTX_LEN = 256
EPS = 1e-6
ROPE_THETA = 10000.0

A_HEADS = 4
A_DH = 64
A_W = A_HEADS * 2 * A_DH
Q_BLOCK = 128
B_GROUPS = 4
B_GW = 128
B_W = B_GROUPS * B_GW
POOL_WINDOWS = (2, 4, 8, 16)
C_HEADS = 8
C_DH = 64
C_W = C_HEADS * C_DH
NA_ROWS = 8
NA_COLS = 16
D_GROUPS = 4
D_GW = 128
D_W = D_GROUPS * D_GW
CHUNK = 128
N_BRANCH = 4
FFN_HIDDEN = ((8 * D_MODEL // 3 + 255) // 256) * 256

OFF_AQ = 0
OFF_AK = OFF_AQ + A_W
OFF_AV = OFF_AK + A_W
OFF_B = OFF_AV + A_W
OFF_CQ = OFF_B + B_W
OFF_CK = OFF_CQ + C_W
OFF_CV = OFF_CK + C_W
OFF_DU = OFF_CV + C_W
OFF_DV = OFF_DU + D_W
OFF_G = OFF_DV + D_W
IN_COLS = OFF_G + N_BRANCH * D_MODEL
BRANCH_WIDTHS = (A_W, B_W, C_W, D_W)
MIX_W = A_W + B_W + C_W + D_W

kernel_name = 'hybrid_gated_branch_diffusion_block'


def _rmsnorm(t, g):
    tf = t.astype(jnp.float32)
    n = tf * lax.rsqrt(jnp.mean(tf * tf, axis=-1, keepdims=True) + EPS)
    return n.astype(t.dtype) * g


def _modulate(h, shift, scale):
    return h * (1 + scale) + shift


def _axial_rope(n_tok, dim, dtype):
    nf = dim // 4
    t = jnp.arange(n_tok)
    row = (t // GRID_W).astype(jnp.float32)
    col = (t % GRID_W).astype(jnp.float32)
    inv = ROPE_THETA ** (-jnp.arange(nf, dtype=jnp.float32) / nf)
    ar = row[:, None] * inv
    ac = col[:, None] * inv
    ang = jnp.concatenate([ar, ar, ac, ac], axis=-1)
    return jnp.cos(ang).astype(dtype), jnp.sin(ang).astype(dtype)


def _rope(t, cos, sin):
    a, b, cc, d = jnp.split(t, 4, axis=-1)
    rot = jnp.concatenate([-b, a, -d, cc], axis=-1)
    return t * cos + rot * sin


def _col_tables():
    ncb = GRID_W // NA_COLS
    qcol = np.arange(GRID_W).reshape(ncb, NA_COLS)
    band0 = np.clip(np.arange(ncb) * NA_COLS - NA_COLS // 2, 0, GRID_W - 2 * NA_COLS)
    band = band0[:, None] + np.arange(2 * NA_COLS)
    win0 = np.clip(qcol - NA_COLS // 2, 0, GRID_W - NA_COLS)
    kcol = band[:, None, :]
    valid = (kcol >= win0[..., None]) & (kcol < win0[..., None] + NA_COLS)
    dc_idx = np.clip(kcol - qcol[..., None] + NA_COLS - 1, 0, 2 * NA_COLS - 2)
    return band, valid, dc_idx


def _heads(t, h, d):
    return t.reshape(t.shape[0], t.shape[1], h, d).transpose(0, 2, 1, 3)


def _merge_heads(t):
    return t.transpose(0, 2, 1, 3).reshape(t.shape[0], t.shape[2], -1)


def _diff_heads(t, g):
    t = t.reshape(t.shape[0], t.shape[1], A_HEADS, 2, A_DH)
    return _rmsnorm(t, g).transpose(0, 2, 3, 1, 4)


def _diff_attend(q, k, v, lam):
    s = jnp.einsum('bhcqd,bhckd->bhcqk', q, k).astype(jnp.float32) * (A_DH ** -0.5)
    p = jax.nn.softmax(s, axis=-1)
    a = p[:, :, 0] - lam * p[:, :, 1]
    return jnp.einsum('bhqk,bhkv->bhqv', a.astype(v.dtype), v)


def _diff_attention_blocks(q, k, v, lam):
    b_, h, _, s, dh = q.shape
    nb = s // Q_BLOCK
    qb = q.reshape(b_, h, 2, nb, Q_BLOCK, dh).transpose(3, 0, 1, 2, 4, 5)
    out = lax.map(lambda qi: _diff_attend(qi, k, v, lam), qb)
    return out.transpose(1, 2, 0, 3, 4).reshape(b_, h, s, 2 * A_DH)


def _dense_attention(q, k, v):
    s = jnp.einsum('bqhd,bkhd->bhqk', q, k).astype(jnp.float32) * (q.shape[-1] ** -0.5)
    p = jax.nn.softmax(s, axis=-1).astype(v.dtype)
    o = jnp.einsum('bhqk,bkhd->bqhd', p, v)
    return o.reshape(o.shape[0], o.shape[1], -1)


def _neighbourhood_attention(q, k, v, k_ctx, v_ctx, rpb, col_tabs):
    band, valid, dc_idx = col_tabs
    b_, s, h, dh = q.shape
    rows = s // GRID_W
    wr = min(NA_ROWS, rows)
    ncb = GRID_W // NA_COLS
    nk = 2 * NA_COLS
    scale = dh ** -0.5
    qr = q.reshape(b_, rows, ncb, NA_COLS, h, dh).transpose(1, 0, 4, 2, 3, 5)

    def grid_band(t):
        t = t.reshape(b_, rows, GRID_W, h, dh).transpose(0, 3, 1, 2, 4)
        return t[:, :, :, band]

    kb_all = grid_band(k)
    vb_all = grid_band(v)
    kc = k_ctx.transpose(0, 2, 1, 3)
    vc = v_ctx.transpose(0, 2, 1, 3)
    col_bias = rpb[:, :, dc_idx]
    n_loc = wr * nk

    def row_step(args):
        r, qrow = args
        rs = jnp.clip(r - wr // 2, 0, rows - wr)
        kb = lax.dynamic_slice_in_dim(kb_all, rs, wr, axis=2)
        vb = lax.dynamic_slice_in_dim(vb_all, rs, wr, axis=2)
        ridx = rs + jnp.arange(wr) - r + NA_ROWS - 1
        bias = jnp.take(col_bias, ridx, axis=1).transpose(0, 2, 3, 1, 4)
        s_loc = jnp.einsum('bhjqd,bhrjkd->bhjqrk', qrow, kb).astype(jnp.float32) * scale
        s_loc = jnp.where(valid[:, :, None, :], s_loc + bias.astype(jnp.float32), -jnp.inf)
        s_ctx = jnp.einsum('bhjqd,bhkd->bhjqk', qrow, kc).astype(jnp.float32) * scale
        sc = jnp.concatenate([s_loc.reshape(b_, h, ncb, NA_COLS, n_loc), s_ctx], axis=-1)
        p = jax.nn.softmax(sc, axis=-1).astype(v.dtype)
        p_loc = p[..., :n_loc].reshape(b_, h, ncb, NA_COLS, wr, nk)
        return (jnp.einsum('bhjqrk,bhrjkd->bhjqd', p_loc, vb)
                + jnp.einsum('bhjqk,bhkd->bhjqd', p[..., n_loc:], vc))

    out = lax.map(row_step, (jnp.arange(rows), qr))
    return out.transpose(1, 0, 3, 4, 2, 5).reshape(b_, s, h * dh)


def _pool_mixer(p, w_pool, s_pool):
    b_, l_, _ = p.shape
    pg = p.reshape(b_, l_, B_GROUPS, B_GW)
    cs = jnp.cumsum(pg.astype(jnp.float32), axis=1)
    cs = jnp.concatenate([jnp.zeros_like(cs[:, :1]), cs], axis=1)
    t = jnp.arange(l_)
    means = []
    for g, w in enumerate(POOL_WINDOWS):
        lo = jnp.clip(t - w // 2, 0, l_)
        hi = jnp.clip(t + w // 2, 0, l_)
        cnt = (hi - lo).astype(jnp.float32)[None, :, None]
        means.append((cs[:, hi, g] - cs[:, lo, g]) / cnt)
    pooled = jnp.stack(means, axis=2).astype(p.dtype) - pg
    y = jnp.einsum('blgc,gcd->blgd', pooled, w_pool)
    return y.reshape(b_, l_, B_W) * s_pool


def _spatial_gating(u, v, vn_g, w_s, b_s):
    b_, l_, _ = u.shape
    v = _rmsnorm(v, vn_g).reshape(b_, l_ // CHUNK, CHUNK, D_GROUPS, D_GW)
    sv = jnp.einsum('gpq,bnqgc->bnpgc', w_s, v) + b_s.T[:, :, None]
    return u * sv.reshape(b_, l_, D_W)


def _merge_branches(ys, gate_pre, w_branch, w_out):
    d = w_out.shape[0]
    acc = None
    off = 0
    for i, (y, w) in enumerate(zip(ys, BRANCH_WIDTHS)):
        term = jax.nn.sigmoid(gate_pre[..., i * d:(i + 1) * d]) * (y @ w_branch[off:off + w])
        acc = term if acc is None else acc + term
        off += w
    return acc @ w_out


def _swiglu(h, w_gu, w_down):
    a, b = jnp.split(h @ w_gu, 2, axis=-1)
    return (jax.nn.silu(a) * b) @ w_down


def setup_inputs(seed: int = 0) -> dict:
    key = jax.random.key(seed)
    ks = jax.random.split(key, 24)
    f32 = jnp.float32
    D = D_MODEL
    L = DEPTH

    def nrm(k, shape, scale):
        return jax.random.normal(k, shape, f32) * scale

    return {
        'x': nrm(ks[0], (BATCH, SEQ, D), 1.0),
        'c': nrm(ks[1], (BATCH, D), 1.0),
        'ctx': nrm(ks[2], (BATCH, CTX_LEN, D), 1.0),
        'c_ctx': nrm(ks[3], (D,), 1.0),
        'w_mod': nrm(ks[4], (L, D, 6 * D), 0.5 * D ** -0.5),
        'b_mod': nrm(ks[5], (L, 6 * D), 0.02),
        'norm1_g': 1.0 + nrm(ks[6], (L, D), 0.02),
        'w_in': nrm(ks[7], (L, D, IN_COLS), D ** -0.5),
        'a_qk_g': 1.0 + nrm(ks[8], (L, 2, A_DH), 0.02),
        'a_lambda': nrm(ks[9], (L, 4, A_DH), 0.1),
        'a_subln_g': 1.0 + nrm(ks[10], (L, 2 * A_DH), 0.02),
        'b_pool_w': nrm(ks[11], (L, B_GROUPS, B_GW, B_GW), B_GW ** -0.5),
        'b_pool_s': 1.0 + nrm(ks[12], (L, B_W), 0.1),
        'c_qk_g': 1.0 + nrm(ks[13], (L, 2, C_DH), 0.02),
        'c_rpb': nrm(ks[14], (L, C_HEADS, 2 * NA_ROWS - 1, 2 * NA_COLS - 1), 0.5),
        'd_vn_g': 1.0 + nrm(ks[15], (L, D_W), 0.02),
        'd_ws': nrm(ks[16], (L, D_GROUPS, CHUNK, CHUNK), CHUNK ** -0.5),
        'd_bs': 1.0 + nrm(ks[17], (L, D_GROUPS, CHUNK), 0.02),
        'w_branch': nrm(ks[18], (L, MIX_W, D), A_W ** -0.5),
        'w_out': nrm(ks[19], (L, D, D), D ** -0.5),
        'norm2_g': 1.0 + nrm(ks[20], (L, D), 0.02),
        'w_gu': nrm(ks[21], (L, D, 2 * FFN_HIDDEN), D ** -0.5),
        'w_down': nrm(ks[22], (L, FFN_HIDDEN, D), FFN_HIDDEN ** -0.5),
    }


def reference(x, c, ctx, c_ctx, w_mod, b_mod, norm1_g, w_in, a_qk_g, a_lambda, a_subln_g,
              b_pool_w, b_pool_s, c_qk_g, c_rpb, d_vn_g, d_ws, d_bs, w_branch, w_out,
              norm2_g, w_gu, w_down):
    s = x.shape[1]
    cos_a, sin_a = _axial_rope(s, A_DH, x.dtype)
    col_tabs = _col_tables()
    for l in range(DEPTH):
        last = l == DEPTH - 1
        lam_init = 0.8 - 0.6 * math.exp(-0.3 * l)
        lam = (jnp.exp(jnp.sum(a_lambda[l, 0] * a_lambda[l, 1]))
               - jnp.exp(jnp.sum(a_lambda[l, 2] * a_lambda[l, 3])) + lam_init)

        mod_x = jax.nn.silu(c) @ w_mod[l] + b_mod[l]
        mod_c = jax.nn.silu(c_ctx) @ w_mod[l] + b_mod[l]
        sh1, sc1, g1, sh2, sc2, g2 = jnp.split(mod_x[:, None, :], 6, axis=-1)
        sh1c, sc1c, g1c, sh2c, sc2c, g2c = jnp.split(mod_c, 6)

        hx = _modulate(_rmsnorm(x, norm1_g[l]), sh1, sc1)
        hc = _modulate(_rmsnorm(ctx, norm1_g[l]), sh1c, sc1c)
        px = hx @ w_in[l]
        if last:
            ccol = lambda off, w: hc @ w_in[l][:, off:off + w]
        else:
            pc = hc @ w_in[l]
            ccol = lambda off, w: pc[..., off:off + w]
        xcol = lambda off, w: px[..., off:off + w]

        ga_q, ga_k = a_qk_g[l, 0], a_qk_g[l, 1]
        aq = _rope(_diff_heads(xcol(OFF_AQ, A_W), ga_q), cos_a, sin_a)
        ak = _rope(_diff_heads(xcol(OFF_AK, A_W), ga_k), cos_a, sin_a)
        av = _heads(xcol(OFF_AV, A_W), A_HEADS, 2 * A_DH)
        ak_c = _diff_heads(ccol(OFF_AK, A_W), ga_k)
        av_c = _heads(ccol(OFF_AV, A_W), A_HEADS, 2 * A_DH)
        k_all = jnp.concatenate([ak, ak_c], axis=3)
        v_all = jnp.concatenate([av, av_c], axis=2)
        ya = _diff_attention_blocks(aq, k_all, v_all, lam)
        ya = _merge_heads(_rmsnorm(ya, a_subln_g[l]) * (1 - lam_init))

        yb = _pool_mixer(xcol(OFF_B, B_W), b_pool_w[l], b_pool_s[l])

        gc_q, gc_k = c_qk_g[l, 0], c_qk_g[l, 1]
        split_c = lambda t: t.reshape(t.shape[0], t.shape[1], C_HEADS, C_DH)
        cq = _rmsnorm(split_c(xcol(OFF_CQ, C_W)), gc_q)
        ck = _rmsnorm(split_c(xcol(OFF_CK, C_W)), gc_k)
        cv = split_c(xcol(OFF_CV, C_W))
        ck_c = _rmsnorm(split_c(ccol(OFF_CK, C_W)), gc_k)
        cv_c = split_c(ccol(OFF_CV, C_W))
        yc = _neighbourhood_attention(cq, ck, cv, ck_c, cv_c, c_rpb[l], col_tabs)

        yd = _spatial_gating(xcol(OFF_DU, D_W), xcol(OFF_DV, D_W), d_vn_g[l], d_ws[l], d_bs[l])

        mix_x = _merge_branches((ya, yb, yc, yd), xcol(OFF_G, N_BRANCH * D_MODEL),
                                w_branch[l], w_out[l])
        x = x + g1 * mix_x
        x = x + g2 * _swiglu(_modulate(_rmsnorm(x, norm2_g[l]), sh2, sc2), w_gu[l], w_down[l])

        if not last:
            aq_c = _diff_heads(ccol(OFF_AQ, A_W), ga_q)
            ya_c = _merge_heads(_rmsnorm(_diff_attend(aq_c, ak_c, av_c, lam), a_subln_g[l]) * (1 - lam_init))
            yb_c = _pool_mixer(ccol(OFF_B, B_W), b_pool_w[l], b_pool_s[l])
            cq_c = _rmsnorm(split_c(ccol(OFF_CQ, C_W)), gc_q)
            yc_c = _dense_attention(cq_c, ck_c, cv_c)
            yd_c = _spatial_gating(ccol(OFF_DU, D_W), ccol(OFF_DV, D_W), d_vn_g[l], d_ws[l], d_bs[l])
            mix_c = _merge_branches((ya_c, yb_c, yc_c, yd_c), ccol(OFF_G, N_BRANCH * D_MODEL),
                                    w_branch[l], w_out[l])
            ctx = ctx + g1c * mix_c
            ctx = ctx + g2c * _swiglu(_modulate(_rmsnorm(ctx, norm2_g[l]), sh2c, sc2c), w_gu[l], w_down[l])
    return x
```

```python
import math
from contextlib import ExitStack

import numpy as np
import concourse.bass as bass
import concourse.mybir as mybir
from concourse.bass_utils import run_bass_kernel_spmd

F32 = mybir.dt.float32
BF16 = mybir.dt.bfloat16
AF = mybir.ActivationFunctionType
ALU = mybir.AluOpType

import os
N_CORES = 8
_CDIV = int(os.environ.get('C_DIV', '1'))
_CFIN = int(os.environ.get('C_FIN', '1'))
D = 1024
S = 2048
CTX = 256
TT = S + CTX
NTT = TT // 128
DEPTH = 2
IN_COLS = 8704
OFF_AQ, OFF_AK, OFF_AV, OFF_B, OFF_CQ, OFF_CK, OFF_CV, OFF_DU, OFF_DV, OFF_G = (
    0, 512, 1024, 1536, 2048, 2560, 3072, 3584, 4096, 4608)
FFN = 2816
EPS = 1e-6
POOL_W = (2, 4, 8, 16)
NSP = 80
C_IDENT, C_BD64, C_RM, C_ROPE, C_CORR, C_SWAP, C_END = 0, 128, 256, 384, 576, 640, 768
NTAB = 25 * 64


class _Eng:
    def __init__(self, name, sem, self_sync=True):
        self.name = name
        self.sem = sem
        self.count = 0
        self.clock = {}
        self.prog = []
        self.self_sync = self_sync


class Sched:
    def __init__(self, sems, dma_sems):
        self.eng = {
            'pe': _Eng('pe', sems['pe'], self_sync=False),
            'act': _Eng('act', sems['act']),
            'dve': _Eng('dve', sems['dve']),
            'pool': _Eng('pool', sems['pool']),
            'sp': _Eng('sp', None),
        }
        self.dma_free = list(dma_sems)
        self.dma_key = {}
        self.last_w = {}
        self.readers = {}
        self.n_wait = 0
        self.n_ops = 0

    def _need(self, e, tok):
        sem, sname, val, clk = tok
        if e.clock.get(sname, 0) >= val:
            return
        if sem is e.sem and not e.self_sync:
            return
        e.prog.append(('wait', sem, val))
        self.n_wait += 1
        newc = dict(e.clock)
        for k, v in clk.items():
            if newc.get(k, 0) < v:
                newc[k] = v
        if newc.get(sname, 0) < val:
            newc[sname] = val
        e.clock = newc

    def _deps(self, e, reads, writes):
        for r in reads:
            t = self.last_w.get(r)
            if t is not None:
                self._need(e, t)
        for w in writes:
            t = self.last_w.get(w)
            if t is not None:
                self._need(e, t)
            for t in self.readers.get(w, ()):
                self._need(e, t)

    def _commit(self, tok, reads, writes):
        for r in reads:
            self.readers.setdefault(r, []).append(tok)
        for w in writes:
            self.last_w[w] = tok
            self.readers[w] = []

    def op(self, eng, fns, reads=(), writes=()):
        e = self.eng[eng]
        if isinstance(fns, tuple):
            fns = [fns]
        self._deps(e, reads, writes)
        e.count += 1
        for f in fns[:-1]:
            e.prog.append(('ins', f, None))
        e.prog.append(('ins', fns[-1], (e.sem, 1)))
        tok = (e.sem, e.name, e.count, e.clock)
        e.last_tok = tok
        self._commit(tok, reads, writes)
        self.n_ops += len(fns)
        return tok

    def dma(self, queue, key, fn, reads=(), writes=()):
        e = self.eng[queue]
        if key not in self.dma_key:
            self.dma_key[key] = [self.dma_free.pop(), 0, None]
        ent = self.dma_key[key]
        if ent[2] is not None:
            self._need(e, ent[2])
        self._deps(e, reads, writes)
        ent[1] += 16
        sname = 'dma_%s' % (key,)
        e.prog.append(('ins', fn, (ent[0], 16)))
        tok = (ent[0], sname, ent[1], e.clock)
        ent[2] = tok
        self._commit(tok, reads, writes)
        self.n_ops += 1
        return tok

    def fence(self):
        toks = [e.last_tok for e in self.eng.values() if getattr(e, 'last_tok', None) is not None]
        toks += [v[2] for v in self.dma_key.values() if v[2] is not None]
        for e in self.eng.values():
            for t in toks:
                self._need(e, t)

    def wait_all(self, eng, toks):
        e = self.eng[eng]
        for t in toks:
            if t is not None:
                self._need(e, t)

    def emit(self, block):
        def run(e, h):
            for it in e.prog:
                if it[0] == 'wait':
                    h.wait_ge(it[1], it[2])
                else:
                    nm, a, kw = it[1]
                    ins = getattr(h, nm)(*a, **kw)
                    if it[2] is not None:
                        ins.then_inc(it[2][0], it[2][1])

        @block.tensor
        def _(h):
            run(self.eng['pe'], h)

        @block.scalar
        def _(h):
            run(self.eng['act'], h)

        @block.vector
        def _(h):
            run(self.eng['dve'], h)

        @block.gpsimd
        def _(h):
            run(self.eng['pool'], h)

        @block.sync
        def _(h):
            run(self.eng['sp'], h)


class _Rot:
    def __init__(self, name, views):
        self.name = name
        self.views = views
        self.i = 0
        self.held = set()

    def get(self):
        n = len(self.views)
        for _ in range(n):
            i = self.i
            self.i = (self.i + 1) % n
            if i not in self.held:
                return self.views[i], (self.name, i)
        raise RuntimeError("rot pool exhausted " + self.name)

    def hold(self, key):
        self.held.add(key[1])

    def release(self, key):
        self.held.discard(key[1])


def I(name, *a, **kw):
    return (name, a, kw)


def build(nb=4, depth=DEPTH, dbg=(), stop_after=None):
    nc = bass.Bass("TRN2", target_bir_lowering=False)
    dbg = set(dbg)
    NJ = nb + 1

    def din(name, shape, dt=F32):
        return nc.dram_tensor(name, list(shape), dt, kind="ExternalInput").ap()

    x_d = din("x", [nb * S, D])
    ctx_d = din("ctx", [nb * CTX, D])
    cT_d = din("cT", [128, 8 * NJ])
    w_mod_d = din("w_mod", [DEPTH, D, 6 * D])
    w_in_d = din("w_in", [DEPTH, D, IN_COLS])
    w_branch_d = din("w_branch", [DEPTH, 2048, D])
    w_out_d = din("w_out", [DEPTH, D, D])
    w_gu_d = din("w_gu", [DEPTH, D, 2 * FFN])
    w_down_d = din("w_down", [DEPTH, FFN, D])
    pool_w_d = din("b_pool_w", [DEPTH, 4, 128, 128])
    wsT_d = din("d_wsT", [DEPTH, 4, 128, 128])
    smallp_d = din("smallp", [DEPTH, 128, NSP])
    vng_d = din("vn_g", [DEPTH, 512])
    dbs_d = din("d_bs", [DEPTH, 512])
    alam_d = din("a_lambda", [DEPTH, 256])
    rpbtab_d = din("rpbtab", [DEPTH, 4, 128, NTAB])
    rpbmask_d = din("rpbmask", [128, NTAB])
    consts_d = din("consts", [128, C_END])
    y_d = nc.dram_tensor("y", [nb * S, D], F32, kind="ExternalOutput").ap()
    ysc_d = nc.dram_tensor("ysc", [2, 128, 4, TT], BF16).ap()
    NWT = 54
    wsc_d = nc.dram_tensor("wsc", [DEPTH, NWT, 128, 2048], BF16).ap()
    dbg_out = {}

    es = ExitStack()
    with es:
        def sb(name, shape, dt):
            return es.enter_context(nc.sbuf_tensor(name, list(shape), dt))

        X = sb("X", [128, NTT, D], F32)
        HT = sb("HT", [128, 8, TT], BF16)
        WR = sb("WR", [128, 5, 2048], BF16)
        TMPF = sb("TMPF", [128, 6, 512], F32)
        TMPB = sb("TMPB", [128, 4, 512], BF16)
        XH = sb("XH", [128, 2, D], BF16)
        ARENA = sb("ARENA", [128, 11392], F32)
        CONS = sb("CONS", [128, C_END], F32)
        IDB = sb("IDB", [128, 128], BF16)
        ONEB = sb("ONEB", [128, 128], BF16)
        BD64B = sb("BD64B", [128, 128], BF16)
        SWAPB = sb("SWAPB", [128, 128], BF16)
        ONEF = sb("ONEF", [128, 128], F32)
        DIAG = sb("DIAG", [128, 2, 128], F32)
        SMALLP = sb("SMALLP", [128, DEPTH, NSP], F32)
        MODT = sb("MODT", [128, DEPTH, 48, NJ], F32)
        SCT = sb("SCT", [128, 8, NJ], BF16)
        CTF = sb("CTF", [128, 8 * NJ], F32)
        MV = sb("MV", [128, 8, 8], F32)
        SS = sb("SS", [128, 2, NTT], F32)
        SS1 = sb("SS1", [128, 8], F32)
        LAM = sb("LAM", [128, DEPTH, 4], F32)
        ALAM = sb("ALAM", [128, 256], F32)
        WPOOL = sb("WPOOL", [128, 4, 128], BF16)
        WST = sb("WST", [128, 4, 128], BF16)
        JUNK = sb("JUNK", [128, D], BF16)

        def carve(off_bytes, shape, dt):
            n = 1
            for d_ in shape[1:]:
                n *= d_
            nbytes = n * (2 if dt == BF16 else 4)
            assert off_bytes % 4 == 0 and off_bytes + nbytes <= 11392 * 4, (off_bytes, shape)
            v = ARENA[:, off_bytes // 4:(off_bytes + nbytes + 3) // 4]
            if dt == BF16:
                v = v.bitcast(BF16)
            if len(shape) == 3:
                v = v.rearrange("p (a b) -> p a b", a=shape[1])
            elif len(shape) == 4:
                v = v.rearrange("p (a b c) -> p a b c", a=shape[1], b=shape[2])
            return v, off_bytes + nbytes

        QKV, o_ = carve(0, [128, 2, 3, TT], BF16)
        TAB, o_ = carve(o_, [128, NTAB], BF16)
        MASK, o_ = carve(o_, [128, NTAB], BF16)
        ROPE, o_ = carve(o_, [128, 2, 512], F32)
        QZ, o_ = carve(o_, [128, 2, 2, 512], BF16)
        GBC, o_ = carve(0, [128, 2, D], F32)
        YT, o_ = carve(o_, [128, 2, 4, 512], BF16)
        YBD, o_ = carve(o_, [128, 2, 4, 512], BF16)
        ACCT, o_ = carve(o_, [128, 8, 512], BF16)
        WX, o_ = carve(o_, [128, 2048], BF16)
        VN, o_ = carve(o_, [128, 4, 512], BF16)
        VNG, o_ = carve(o_, [128, 512], F32)
        DBS, o_ = carve(o_, [128, 512], F32)
        ACTT, _o2 = carve(8192, [128, 2, 512], BF16)

        PS = [es.enter_context(nc.psum_tensor("PS%d" % i, [128, 512], F32)) for i in range(8)]
        sems = {k: es.enter_context(nc.semaphore("sem_" + k)) for k in ['pe', 'act', 'dve', 'pool']}
        dsems = [es.enter_context(nc.semaphore("dsem%d" % i)) for i in range(72)]
        block = es.enter_context(nc.Block())
        Sc = Sched(sems, dsems)

        psr = _Rot('ps', [p[:] for p in PS])
        tmpf = _Rot('tmpf', [TMPF[:, i, :] for i in range(6)])
        tmpb = _Rot('tmpb', [TMPB[:, i, :] for i in range(4)])
        wring = _Rot('W', [WR[:, i, :] for i in range(5)] + [WX])
        wring.held.add(5)
        xh = _Rot('xh', [XH[:, i, :] for i in range(2)])
        diag = _Rot('diag', [DIAG[:, i, :] for i in range(2)])

        ident_f = CONS[:, C_IDENT:C_IDENT + 128]
        bd64_f = CONS[:, C_BD64:C_BD64 + 128]
        rm_f = CONS[:, C_RM:C_RM + 128]
        MUL, ADD, SUB = ALU.mult, ALU.add, ALU.subtract

        def wload(src, a, b):
            v, key = wring.get()
            view = v[:, 0:a * b].rearrange("p (a b) -> p a b", a=a)
            Sc.dma('pool', key, I('dma_start', out=view, in_=src), writes=[key, (key, 1)])
            return view, key

        def wload_t(l, t, a, b):
            v, key = wring.get()
            view = v[:, 0:a * b].rearrange("p (a b) -> p a b", a=a)
            Sc.dma('sp', key, I('dma_start', out=view, in_=wsc_d[l, t, :, 0:a * b].rearrange("p (a b) -> p a b", a=a)), writes=[key, (key, 1)])
            return view, key

        def win_cols(l, c0, n):
            return w_in_d[l, :, c0:c0 + n].rearrange("(k p) n -> p k n", p=128)

        def dump(name, ap_sb, shape, dt, reads):
            if name not in dbg:
                return
            t = nc.dram_tensor("dbg_" + name, list(shape), dt, kind="ExternalOutput").ap()
            dbg_out[name] = Sc.dma('sp', 'dbg_' + name, I('dma_start', out=t, in_=ap_sb), reads=reads)

        Sc.dma('sp', 'c0', I('dma_start', out=CONS[:], in_=consts_d), writes=['CONS'])
        Sc.dma('sp', 'c1', I('dma_start', out=SMALLP[:], in_=smallp_d.rearrange("l p n -> p l n")), writes=['SMALLP'])
        Sc.dma('sp', 'c2', I('dma_start', out=CTF[:], in_=cT_d), writes=['CTF'])
        Sc.op('dve', I('tensor_copy', out=IDB[:], in_=ident_f), reads=['CONS'], writes=['IDB'])
        Sc.op('dve', I('tensor_copy', out=BD64B[:], in_=bd64_f), reads=['CONS'], writes=['BD64B'])
        Sc.op('dve', I('tensor_copy', out=SWAPB[:], in_=CONS[:, C_SWAP:C_SWAP + 128]), reads=['CONS'], writes=['SWAPB'])
        Sc.op('dve', I('memset', ONEB[:], 1.0), writes=['ONEB'])
        Sc.op('dve', I('memset', ONEF[:], 1.0), writes=['ONEF'])
        Sc.op('act', I('activation', out=SCT[:].rearrange("p k j -> p (k j)"), in_=CTF[:], func=AF.Silu), reads=['CTF'], writes=['SCT'])

        pci = [0]

        def precast(l, t, src, a, b):
            Sc.dma('pool', ('pc', pci[0] % 8), I('dma_start', out=wsc_d[l, t, :, 0:a * b].rearrange("p (a b) -> p a b", a=a), in_=src))
            pci[0] += 1
        for l in range(depth):
            for j in range(8):
                for i_ in range(4):
                    precast(l, j * 4 + i_, win_cols(l, OFF_G + i_ * 1024 + j * 128, 128), 8, 128)
                precast(l, 32 + j, w_branch_d[l, :, j * 128:(j + 1) * 128].rearrange("(k p) n -> p k n", p=128), 16, 128)
            for pg in range(4):
                precast(l, 40 + pg, win_cols(l, OFF_B + pg * 128, 128), 8, 128)
                precast(l, 44 + pg, win_cols(l, OFF_DU + pg * 128, 128), 8, 128)
                precast(l, 50 + pg, w_out_d[l, :, pg * 256:(pg + 1) * 256].rearrange("(k p) n -> p k n", p=128), 8, 256)
            for hh in range(2):
                precast(l, 48 + hh, win_cols(l, OFF_DV + hh * 256, 256), 8, 256)

        for l in range(depth):
            lam_init = 0.8 - 0.6 * math.exp(-0.3 * l)
            Sc.dma('sp', 'c6', I('dma_start', out=ALAM[:], in_=alam_d[l].partition_broadcast(128)), writes=['ALAM'])
            Sc.op('dve', I('tensor_tensor', out=ALAM[:, 0:64], in0=ALAM[:, 0:64], in1=ALAM[:, 64:128], op=MUL), reads=['ALAM'], writes=['ALAM'])
            Sc.op('dve', I('tensor_tensor', out=ALAM[:, 128:192], in0=ALAM[:, 128:192], in1=ALAM[:, 192:256], op=MUL), reads=['ALAM'], writes=['ALAM'])
            Sc.op('dve', I('reduce_sum', out=SS1[:, 0:1], in_=ALAM[:, 0:64], axis=mybir.AxisListType.X), reads=['ALAM'], writes=['SS1'])
            Sc.op('dve', I('reduce_sum', out=SS1[:, 1:2], in_=ALAM[:, 128:192], axis=mybir.AxisListType.X), reads=['ALAM', 'SS1'], writes=['SS1'])
            Sc.op('act', I('activation', out=SS1[:, 2:4], in_=SS1[:, 0:2], func=AF.Exp), reads=['SS1'], writes=['SS1'])
            Sc.op('dve', I('tensor_tensor', out=LAM[:, l, 0:1], in0=SS1[:, 2:3], in1=SS1[:, 3:4], op=SUB), reads=['SS1'], writes=[('LAM', l)])
            Sc.op('dve', I('tensor_scalar', out=LAM[:, l, 0:1], in0=LAM[:, l, 0:1], scalar1=lam_init, scalar2=None, op0=ADD), reads=[('LAM', l)], writes=[('LAM', l)])
            Sc.op('dve', I('tensor_scalar', out=LAM[:, l, 1:2], in0=LAM[:, l, 0:1], scalar1=-1.0, scalar2=None, op0=MUL), reads=[('LAM', l)], writes=[('LAM', l)])
            Sc.op('dve', I('tensor_scalar', out=LAM[:, l, 2:3], in0=SMALLP[:, l, 68:69], scalar1=1.0 - lam_init, scalar2=None, op0=MUL), reads=['SMALLP', ('LAM', l)], writes=[('LAM', l)])

        for l in range(depth):
            pm, pmk = psr.get()
            for blk in range(24):
                wv, wk = wload(w_mod_d[l, :, blk * 256:(blk + 1) * 256].rearrange("(k p) n -> p k n", p=128), 8, 256)
                fns = []
                for m in range(2):
                    ch = blk * 2 + m
                    for k in range(8):
                        fns.append(I('matmul', pm[:, ch * NJ:(ch + 1) * NJ], lhsT=wv[:, k, m * 128:(m + 1) * 128], rhs=SCT[:, k, :],
                                     start=(k == 0), stop=(k == 7)))
                Sc.op('pe', fns, reads=[wk, 'SCT'], writes=[pmk])
            Sc.op('dve', I('tensor_tensor', out=MODT[:, l, :, :], in0=pm[:, 0:48 * NJ].rearrange("p (c j) -> p c j", j=NJ),
                           in1=SMALLP[:, l, 0:48].unsqueeze(2).to_broadcast([128, 48, NJ]), op=ADD),
                  reads=[pmk, 'SMALLP'], writes=[('MODT', l)])
        dump("modT", MODT[:], [128, DEPTH, 48, NJ], F32, [('MODT', l) for l in range(depth)])
        Sc.fence()

        GROUPS = [(0, 512), (512, 512), (1024, 512), (1536, 512), (2048, 256)]

        def ht_keys(g):
            return [(nm, g, i) for i in range(4 if g < 4 else 2) for nm in ('HT', 'HTb')]

        def all_ht_keys():
            return [k for g in range(5) for k in ht_keys(g)]

        def norm_stats(which):
            Sc.op('dve', I('memset', SS[:, which, :], 0.0), writes=[('SS', which)])
            for tt in range(NTT):
                Sc.op('act', I('activation', out=JUNK[:], in_=X[:, tt, :], func=AF.Square, accum_out=SS[:, which, tt:tt + 1]),
                      reads=[('X', tt)], writes=['JUNK', ('SS', which)])
            Sc.op('act', I('activation', out=SS[:, which, :], in_=SS[:, which, :], func=AF.Sqrt, scale=1.0 / D, bias=EPS), reads=[('SS', which)], writes=[('SS', which)])
            Sc.op('dve', I('reciprocal', out=SS[:, which, :], in_=SS[:, which, :]), reads=[('SS', which)], writes=[('SS', which)])

        def ffn_ht(g):
            t0, n = GROUPS[g]
            tts = list(range(t0 // 128, (t0 + n) // 128))
            a_, b_ = tts[0], tts[-1] + 1
            Sc.op('dve', I('memset', SS[:, 1, a_:b_], 0.0), writes=[('SS2', g)])
            for tt in tts:
                Sc.op('act', I('activation', out=JUNK[:], in_=X[:, tt, :], func=AF.Square, accum_out=SS[:, 1, tt:tt + 1]),
                      reads=[('X', tt), ('SS2', g)], writes=['JUNK', ('SS2', g)])
            Sc.op('act', I('activation', out=SS[:, 1, a_:b_], in_=SS[:, 1, a_:b_], func=AF.Sqrt, scale=1.0 / D, bias=EPS), reads=[('SS2', g)], writes=[('SS2', g)])
            Sc.op('dve', I('reciprocal', out=SS[:, 1, a_:b_], in_=SS[:, 1, a_:b_]), reads=[('SS2', g)], writes=[('SS2', g)])
            make_ht(1, 4, tts, sskey=('SS2', g))

        def make_ht(which, mva, tts, sskey=None):
            for tt in tts:
                a_i = mva + 2 if tt >= 16 else mva
                xv, xk = xh.get()
                Sc.op('dve', I('tensor_scalar', out=xv, in0=X[:, tt, :], scalar1=SS[:, which, tt:tt + 1], scalar2=None, op0=MUL),
                      reads=[('X', tt), sskey or ('SS', which)], writes=[xk])
                pt, ptk = psr.get()
                ptb = pt.bitcast(BF16)
                Sc.op('pe', [I('transpose', out=ptb[:, k * 128:(k + 1) * 128], in_=xv[:, k * 128:(k + 1) * 128], identity=IDB[:]) for k in range(8)],
                      reads=[xk, 'IDB'], writes=[ptk])
                g = min(tt // 4, 4)
                Sc.op('act', [I('activation', out=HT[:, k, tt * 128:(tt + 1) * 128], in_=ptb[:, k * 128:(k + 1) * 128], func=AF.Identity,
                                scale=MV[:, a_i, k:k + 1], bias=MV[:, a_i + 1, k:k + 1]) for k in range(8)],
                      reads=[ptk, 'MV'], writes=[('HT', g, tt % 4)])

        def gate_bc(l, mod_chunk0, slot, jcol):
            for half in range(2):
                pg_, pgk = psr.get()
                for kk in range(4):
                    k = half * 4 + kk
                    dv, dk = diag.get()
                    Sc.op('dve', I('tensor_scalar', out=dv, in0=ident_f, scalar1=MODT[:, l, mod_chunk0 + k, jcol:jcol + 1], scalar2=None, op0=MUL),
                          reads=['CONS', ('MODT', l)], writes=[dk])
                    Sc.op('pe', I('matmul', pg_[:, kk * 128:(kk + 1) * 128], lhsT=ONEF[:], rhs=dv, start=True, stop=True),
                          reads=[dk, 'ONEF'], writes=[pgk])
                Sc.op('act', I('activation', out=GBC[:, slot, half * 512:(half + 1) * 512], in_=pg_, func=AF.Copy),
                      reads=[pgk], writes=[('GBC', slot)])

        def qknorm_gen(ps_raw, psk, n, gcol, rope, out_ap, out_keys):
            sq, sqk = tmpf.get()
            Sc.op('act', I('activation', out=sq[:, 0:n], in_=ps_raw[:, 0:n], func=AF.Square), reads=[psk], writes=[sqk])
            yield
            p2, p2k = psr.get()
            Sc.op('pe', I('matmul', p2[:, 0:n], lhsT=bd64_f, rhs=sq[:, 0:n], start=True, stop=True), reads=[sqk, 'CONS'], writes=[p2k])
            yield
            rs, rsk = tmpf.get()
            Sc.op('act', I('activation', out=rs[:, 0:n], in_=p2[:, 0:n], func=AF.Sqrt, scale=1.0 / 64, bias=EPS), reads=[p2k], writes=[rsk])
            yield
            Sc.op('dve', I('reciprocal', out=rs[:, 0:n], in_=rs[:, 0:n]), reads=[rsk], writes=[rsk])
            yield
            if not rope:
                Sc.op('dve', I('scalar_tensor_tensor', out=out_ap, in0=ps_raw[:, 0:n], scalar=gcol, in1=rs[:, 0:n], op0=MUL, op1=MUL),
                      reads=[psk, rsk, 'SMALLP'], writes=out_keys)
                return
            nn, nk = tmpf.get()
            Sc.op('dve', I('scalar_tensor_tensor', out=nn[:, 0:n], in0=ps_raw[:, 0:n], scalar=gcol, in1=rs[:, 0:n], op0=MUL, op1=MUL),
                  reads=[psk, rsk, 'SMALLP'], writes=[nk])
            yield
            p3, p3k = psr.get()
            Sc.op('pe', I('matmul', p3[:, 0:n], lhsT=rm_f, rhs=nn[:, 0:n], start=True, stop=True), reads=[nk, 'CONS'], writes=[p3k])
            Sc.op('dve', I('tensor_tensor', out=sq[:, 0:n], in0=nn[:, 0:n], in1=ROPE[:, 0, 0:n], op=MUL), reads=[nk, 'ROPE'], writes=[sqk])
            yield
            Sc.op('dve', I('tensor_tensor', out=rs[:, 0:n], in0=p3[:, 0:n], in1=ROPE[:, 1, 0:n], op=MUL), reads=[p3k, 'ROPE'], writes=[rsk])
            yield
            Sc.op('dve', I('tensor_tensor', out=out_ap, in0=sq[:, 0:n], in1=rs[:, 0:n], op=ADD), reads=[sqk, rsk], writes=out_keys)

        def lockstep(gens):
            gens = list(gens)
            while gens:
                nxt_ = []
                for g_ in gens:
                    try:
                        next(g_)
                        nxt_.append(g_)
                    except StopIteration:
                        pass
                gens = nxt_

        def qknorm(*a):
            lockstep([qknorm_gen(*a)])

        def rope_tables(g):
            for cs in range(2):
                base = C_ROPE + cs * 96
                Sc.op('dve', I('tensor_tensor', out=ROPE[:, cs, :].rearrange("p (r c) -> p r c", c=64),
                               in0=CONS[:, base + g * 8: base + g * 8 + 8].unsqueeze(2).to_broadcast([128, 8, 64]),
                               in1=CONS[:, base + 32: base + 96].unsqueeze(1).to_broadcast([128, 8, 64]), op=ADD),
                      reads=['CONS'], writes=['ROPE'])

        def proj_fm(wv, wk, g, col0=0, nk=8, rhs_src=None, rhs_keys=None):
            t0, n = GROUPS[g]
            ps, psk = psr.get()
            Sc.op('pe', [I('matmul', ps[:, 0:n], lhsT=wv[:, k, col0:col0 + 128], rhs=HT[:, k, t0:t0 + n], start=(k == 0), stop=(k == nk - 1))
                         for k in range(nk)], reads=[wk] + ht_keys(g), writes=[psk])
            return ps, psk

        def done():
            Sc.wait_all('sp', list(dbg_out.values()))
            Sc.wait_all('sp', [v[2] for k, v in Sc.dma_key.items() if isinstance(k, str) and k.startswith('yout')])
            Sc.emit(block)
            build.last_sched = Sc
            print("sched: ops", Sc.n_ops, "waits", Sc.n_wait, "dma keys", len(Sc.dma_key),
                  "per-engine", {k: len(v.prog) for k, v in Sc.eng.items()})

        for b in range(nb):
            for q in range(4):
                Sc.dma('sp', 'xin%d' % q, I('dma_start', out=X[:, q * 4:(q + 1) * 4, :],
                                            in_=x_d[b * S + q * 512: b * S + (q + 1) * 512, :].rearrange("(t p) d -> p t d", p=128)),
                       writes=[('X', q * 4 + i) for i in range(4)])
            Sc.dma('sp', 'xin4', I('dma_start', out=X[:, 16:18, :], in_=ctx_d[b * CTX:(b + 1) * CTX, :].rearrange("(t p) d -> p t d", p=128)),
                   writes=[('X', 16), ('X', 17)])
            for l in range(depth):
                last = (l == DEPTH - 1)
                ngrp = 4 if last else 5
                for (row, gcol, sc_ch, sh_ch, jcol) in ((0, 48, 8, 0, b), (2, 48, 8, 0, nb), (4, 56, 32, 24, b), (6, 56, 32, 24, nb)):
                    Sc.op('dve', I('scalar_tensor_tensor', out=MV[:, row, :], in0=MODT[:, l, sc_ch:sc_ch + 8, jcol], scalar=1.0,
                                   in1=SMALLP[:, l, gcol:gcol + 8], op0=ADD, op1=MUL),
                          reads=[('MODT', l), 'SMALLP'], writes=['MV'])
                    Sc.op('dve', I('tensor_copy', out=MV[:, row + 1, :], in_=MODT[:, l, sh_ch:sh_ch + 8, jcol]),
                          reads=[('MODT', l)], writes=['MV'])
                norm_stats(0)
                make_ht(0, 0, range(NTT))
                Sc.fence()
                Sc.dma('pool', 'c3', I('dma_start', out=MASK, in_=rpbmask_d), writes=['MASK'])
                Sc.op('pool', I('memset', QZ[64:128, :, 0, :], 0.0), writes=[('QZ', 0), ('QZ', 1)])
                Sc.op('pool', I('memset', QZ[0:64, :, 1, :], 0.0), writes=[('QZ', 0), ('QZ', 1)])
                qzi = 0
                dump("ht%d" % l, HT[:], [128, 8, TT], BF16, all_ht_keys())
                if stop_after == "ht":
                    done()
                    return nc, list(dbg_out.keys())

                for h in range(4):
                    s = h % 2
                    QA, KA, VA = QKV[:, s, 0, :], QKV[:, s, 1, :], QKV[:, s, 2, :]
                    wq, wqk = wload(win_cols(l, OFF_AQ + h * 128, 128), 8, 128)
                    wk_, wkk = wload(win_cols(l, OFF_AK + h * 128, 128), 8, 128)
                    wv_, wvk = wload(win_cols(l, OFF_AV + h * 128, 128), 8, 128)
                    for g in range(5):
                        t0, n = GROUPS[g]
                        if g < 4:
                            rope_tables(g)
                        gens = []
                        psK, psKk = proj_fm(wk_, wkk, g)
                        gens.append(qknorm_gen(psK, psKk, n, SMALLP[:, l, 65:66], g < 4, KA[:, t0:t0 + n], [('K', s, g)]))
                        if g < ngrp:
                            psQ, psQk = proj_fm(wq, wqk, g)
                            gens.append(qknorm_gen(psQ, psQk, n, SMALLP[:, l, 64:65], g < 4, QA[:, t0:t0 + n], [('Q', s, g)]))
                        ps, psk = psr.get()
                        fns = []
                        for sub in range(n // 128):
                            for k in range(8):
                                fns.append(I('matmul', ps[:, sub * 128:(sub + 1) * 128], lhsT=HT[:, k, t0 + sub * 128: t0 + (sub + 1) * 128],
                                             rhs=wv_[:, k, :], start=(k == 0), stop=(k == 7)))
                        Sc.op('pe', fns, reads=[wvk] + ht_keys(g), writes=[psk])
                        lockstep(gens)
                        Sc.op('act', I('activation', out=VA[:, t0:t0 + n], in_=ps[:, 0:n], func=AF.Copy), reads=[psk], writes=[('V', s, g)])
                    if h == 0:
                        dump("aq%d" % l, QA, [128, TT], BF16, [('Q', s, g) for g in range(ngrp)])
                        dump("ak%d" % l, KA, [128, TT], BF16, [('K', s, g) for g in range(5)])
                        dump("av%d" % l, VA, [128, TT], BF16, [('V', s, g) for g in range(5)])
                        if stop_after == "aqkv":
                            done()
                            return nc, list(dbg_out.keys())
                    for qg in range(ngrp):
                        q0, nq = GROUPS[qg]
                        kcs = list(range(18)) if qg < 4 else [16, 17]
                        steps = [(kc, c) for kc in kcs for c in range(2)]
                        zb = qzi % 2
                        qzi += 1
                        Sc.op('pool', [I('tensor_copy', out=QZ[0:64, zb, 0, 0:nq], in_=QA[0:64, q0:q0 + nq]),
                                       I('tensor_copy', out=QZ[64:128, zb, 1, 0:nq], in_=QA[64:128, q0:q0 + nq])],
                              reads=[('Q', s, qg)], writes=[('QZ', zb)])
                        accs = []
                        for _ in range(2):
                            a_, ak_ = psr.get()
                            psr.hold(ak_)
                            accs.append((a_, ak_))
                        eacc = []
                        for _ in range(2):
                            a_, ak_ = tmpf.get()
                            tmpf.hold(ak_)
                            eacc.append((a_, ak_))
                        s2p, s2pk = psr.get()
                        psr.hold(s2pk)

                        def emit_s(kc, c):
                            pS, pSk = psr.get()
                            Sc.op('pe', I('matmul', pS[:, 0:nq], lhsT=KA[:, kc * 128:(kc + 1) * 128], rhs=QZ[:, zb, c, 0:nq], start=True, stop=True),
                                  reads=[('K', s, kc // 4), ('QZ', zb)], writes=[pSk])
                            return pS, pSk
                        LA = 2
                        pend = [emit_s(*steps[i_]) for i_ in range(min(LA, len(steps)))]
                        for si, (kc, c) in enumerate(steps):
                            pS, pSk = pend.pop(0)
                            if si + LA < len(steps):
                                pend.append(emit_s(*steps[si + LA]))
                            ev, evk = tmpb.get()
                            Sc.op('act', I('activation', out=ev[:, 0:nq], in_=pS[:, 0:nq], func=AF.Exp, scale=0.125), reads=[pSk], writes=[evk])
                            first = (kc == kcs[0])
                            lastk = (kc == kcs[-1])
                            o_, ok_ = accs[c]
                            ea, eak = eacc[c]
                            if c == 0:
                                if first:
                                    Sc.op('pool', I('tensor_copy', out=ea[:, 0:nq], in_=ev[:, 0:nq]), reads=[evk], writes=[eak])
                                else:
                                    Sc.op('pool', I('tensor_tensor', out=ea[:, 0:nq], in0=ea[:, 0:nq], in1=ev[:, 0:nq], op=ADD), reads=[evk, eak], writes=[eak])
                                Sc.op('pe', I('matmul', o_[:, 0:nq], lhsT=VA[:, kc * 128:(kc + 1) * 128], rhs=ev[:, 0:nq], start=first, stop=lastk),
                                      reads=[evk, ('V', s, kc // 4)], writes=[ok_])
                            else:
                                Sc.op('pe', [I('matmul', o_[:, 0:nq], lhsT=VA[:, kc * 128:(kc + 1) * 128], rhs=ev[:, 0:nq], start=first, stop=lastk),
                                             I('matmul', s2p[:, 0:nq], lhsT=ONEB[:], rhs=ev[:, 0:nq], start=first, stop=lastk)],
                                      reads=[evk, ('V', s, kc // 4), 'ONEB'], writes=[ok_, s2pk])
                        (o1, o1k), (o2, o2k) = accs
                        rr = []
                        for c in range(2):
                            ea, eak = eacc[c]
                            if c == 0:
                                p2, p2k = psr.get()
                                Sc.op('pe', I('matmul', p2[:, 0:nq], lhsT=ONEF[:], rhs=ea[:, 0:nq], start=True, stop=True), reads=[eak, 'ONEF'], writes=[p2k])
                            else:
                                p2, p2k = s2p, s2pk
                            Sc.op('dve', I('reciprocal', out=ea[:, 0:nq], in_=p2[:, 0:nq]), reads=[p2k], writes=[eak])
                            rr.append((ea, eak))
                        psr.release(s2pk)
                        (r1, r1k), (r2, r2k) = rr
                        Sc.op('dve', I('tensor_tensor', out=r1[:, 0:nq], in0=o1[:, 0:nq], in1=r1[:, 0:nq], op=MUL), reads=[o1k, r1k], writes=[r1k])
                        Sc.op('dve', I('tensor_tensor', out=r2[:, 0:nq], in0=o2[:, 0:nq], in1=r2[:, 0:nq], op=MUL), reads=[o2k, r2k], writes=[r2k])
                        for _, k_ in accs:
                            psr.release(k_)
                        yp, ypk = tmpf.get()
                        Sc.op('dve', I('scalar_tensor_tensor', out=yp[:, 0:nq], in0=r2[:, 0:nq], scalar=LAM[:, l, 1:2], in1=r1[:, 0:nq], op0=MUL, op1=ADD),
                              reads=[r1k, r2k, ('LAM', l)], writes=[ypk])
                        Sc.op('act', I('activation', out=r1[:, 0:nq], in_=yp[:, 0:nq], func=AF.Square), reads=[ypk], writes=[r1k])
                        p2, p2k = psr.get()
                        Sc.op('pe', I('matmul', p2[:, 0:nq], lhsT=ONEF[:], rhs=r1[:, 0:nq], start=True, stop=True), reads=[r1k, 'ONEF'], writes=[p2k])
                        Sc.op('act', I('activation', out=r2[:, 0:nq], in_=p2[:, 0:nq], func=AF.Sqrt, scale=1.0 / 128, bias=EPS), reads=[p2k], writes=[r2k])
                        Sc.op('dve', I('reciprocal', out=r2[:, 0:nq], in_=r2[:, 0:nq]), reads=[r2k], writes=[r2k])
                        st, stk = tmpb.get()
                        Sc.op('dve', I('scalar_tensor_tensor', out=st[:, 0:nq], in0=yp[:, 0:nq], scalar=LAM[:, l, 2:3], in1=r2[:, 0:nq], op0=MUL, op1=MUL),
                              reads=[ypk, r2k, ('LAM', l)], writes=[stk])
                        for _, k_ in eacc:
                            tmpf.release(k_)
                        Sc.dma('sp', ('yst', stk[1]), I('dma_start', out=ysc_d[0, :, h, q0:q0 + nq], in_=st[:, 0:nq]),
                               reads=[stk], writes=[('ysc', 0, h, qg)])
                dump("ya%d" % l, ysc_d[0], [128, 4, TT], BF16, [('ysc', 0, h, g) for h in range(4) for g in range(ngrp)])
                if stop_after == "ya":
                    done()
                    return nc, list(dbg_out.keys())

                Sc.fence()
                for j in range(4):
                    s = j % 2
                    QC, KC, V2 = QKV[:, s, 0, :], QKV[:, s, 1, :], QKV[:, s, 2, :]
                    V2v = V2.rearrange("p (r c) -> p r c", c=64)
                    wq, wqk = wload(win_cols(l, OFF_CQ + j * 128, 128), 8, 128)
                    wk_, wkk = wload(win_cols(l, OFF_CK + j * 128, 128), 8, 128)
                    wv_, wvk = wload(win_cols(l, OFF_CV + j * 128, 128), 8, 128)
                    for pc in range(4):
                        tf, tfk = tmpf.get()
                        Sc.dma('sp', ('tabld', tfk[1]), I('dma_start', out=tf[:, 0:400], in_=rpbtab_d[l, j, :, pc * 400:(pc + 1) * 400]), writes=[tfk])
                        Sc.op('act', I('activation', out=tf[:, 0:400], in_=tf[:, 0:400], func=AF.Exp), reads=[tfk], writes=[tfk])
                        Sc.op('dve', I('tensor_tensor', out=TAB[:, pc * 400:(pc + 1) * 400], in0=tf[:, 0:400], in1=MASK[:, pc * 400:(pc + 1) * 400], op=MUL),
                              reads=[tfk, 'MASK'], writes=['TAB'])
                    for g in range(5):
                        t0, n = GROUPS[g]
                        nsub = n // 128
                        gens = []
                        psK, psKk = proj_fm(wk_, wkk, g)
                        gens.append(qknorm_gen(psK, psKk, n, SMALLP[:, l, 67:68], False, KC[:, t0:t0 + n], [('K', s, g)]))
                        if g < ngrp:
                            psQ, psQk = proj_fm(wq, wqk, g)
                            gens.append(qknorm_gen(psQ, psQk, n, SMALLP[:, l, 66:67], False, QC[:, t0:t0 + n], [('Q', s, g)]))
                        ps, psk = psr.get()
                        fns = []
                        for sub in range(nsub):
                            for k in range(8):
                                fns.append(I('matmul', ps[:, sub * 128:(sub + 1) * 128], lhsT=HT[:, k, t0 + sub * 128: t0 + (sub + 1) * 128],
                                             rhs=wv_[:, k, :], start=(k == 0), stop=(k == 7)))
                        Sc.op('pe', fns, reads=[wvk] + ht_keys(g), writes=[psk])
                        lockstep(gens)
                        vt, vtk = tmpb.get()
                        Sc.op('act', I('activation', out=vt[:, 0:n], in_=ps[:, 0:n], func=AF.Copy), reads=[psk], writes=[vtk])
                        ps2, ps2k = psr.get()
                        Sc.op('pe', [I('matmul', ps2[:, sub * 128:(sub + 1) * 128], lhsT=SWAPB[:], rhs=vt[:, sub * 128:(sub + 1) * 128], start=True, stop=True)
                                     for sub in range(nsub)], reads=[vtk, 'SWAPB'], writes=[ps2k])
                        r0 = t0 // 64
                        psv = ps[:, 0:n].rearrange("p (s c) -> p s c", c=128)
                        ps2v = ps2[:, 0:n].rearrange("p (s c) -> p s c", c=128)
                        V2g = V2v[:, r0:r0 + 2 * nsub, :].rearrange("p (s two) c -> p s two c", two=2)
                        Sc.op('dve', I('tensor_copy', out=V2g[0:64, :, 0, :], in_=psv[0:64, :, 0:64]), reads=[psk], writes=[('V', s, g, 0)])
                        Sc.op('act', I('activation', out=V2g[64:128, :, 1, :], in_=psv[64:128, :, 64:128], func=AF.Copy), reads=[psk], writes=[('V', s, g, 1)])
                        Sc.op('dve', I('tensor_copy', out=V2g[64:128, :, 0, :], in_=ps2v[64:128, :, 64:128]), reads=[ps2k], writes=[('V', s, g, 2)])
                        Sc.op('act', I('activation', out=V2g[0:64, :, 1, :], in_=ps2v[0:64, :, 0:64], func=AF.Copy), reads=[ps2k], writes=[('V', s, g, 3)])
                    if j == 0:
                        dump("cq%d" % l, QC, [128, TT], BF16, [('Q', s, g) for g in range(ngrp)])
                        dump("ck%d" % l, KC, [128, TT], BF16, [('K', s, g) for g in range(5)])
                        dump("cv%d" % l, V2, [128, TT], BF16, [('V', s, g, q) for g in range(5) for q in range(4)])
                    blocks = [(r, 1, list(range(0, 8))) for r in range(4)]
                    blocks += [(r, 2, list(range(r - 4, r + 5))) for r in range(4, 28, 2)]
                    blocks += [(r, 1, list(range(24, 32))) for r in range(28, 32)]
                    if not last:
                        blocks += [(32, 4, [])]
                    stage = None
                    for (r0, nr, krows) in blocks:
                        nq = 64 * nr
                        q0 = r0 * 64
                        qg = min(r0 // 8, 4)
                        if r0 % 8 == 0:
                            stage, stagek = tmpb.get()
                            tmpb.hold(stagek)
                        O_, Ok_ = psr.get()
                        psr.hold(Ok_)
                        Sm, Smk = psr.get()
                        psr.hold(Smk)
                        bsz = max(1, (512 // nq) // _CDIV)
                        loc = list(reversed(krows))
                        batches = [(loc[i:i + bsz], True) for i in range(0, len(loc), bsz)]
                        batches += [([32, 33, 34, 35][i:i + bsz], False) for i in range(0, 4, bsz)]
                        nsteps = sum(len(bt[0]) for bt in batches)

                        def emit_s(rows):
                            pS, pSk = psr.get()
                            fns = []
                            for i_, kr in enumerate(rows):
                                fns.append(I('matmul', pS[0:64, i_ * nq:(i_ + 1) * nq], lhsT=KC[0:64, kr * 64:(kr + 1) * 64], rhs=QC[0:64, q0:q0 + nq], start=True, stop=True))
                                fns.append(I('matmul', pS[64:128, i_ * nq:(i_ + 1) * nq], lhsT=KC[64:128, kr * 64:(kr + 1) * 64], rhs=QC[64:128, q0:q0 + nq], start=True, stop=True))
                            Sc.op('pe', fns, reads=list({('K', s, kr // 8) for kr in rows}) + [('Q', s, qg)], writes=[pSk])
                            return pS, pSk
                        LA = 2
                        pend = [emit_s(batches[i_][0]) for i_ in range(min(LA, len(batches)))]
                        sdone = 0
                        esum, esumk = tmpf.get()
                        tmpf.hold(esumk)
                        w0 = 0
                        for bi, (rows, is_loc) in enumerate(batches):
                            pS, pSk = pend.pop(0)
                            if bi + LA < len(batches):
                                pend.append(emit_s(batches[bi + LA][0]))
                            nb_ = len(rows)
                            ev, evk = tmpb.get()
                            Sc.op('act', I('activation', out=ev[:, 0:nb_ * nq], in_=pS[:, 0:nb_ * nq], func=AF.Exp, scale=0.125), reads=[pSk], writes=[evk])
                            if is_loc:
                                dr = rows[0] - r0
                                idx = (4 - dr) if nr == 2 else (17 - dr)
                                if nb_ == 1:
                                    Sc.op('dve', I('tensor_tensor', out=ev[:, 0:nq], in0=ev[:, 0:nq], in1=TAB[:, idx * 64: idx * 64 + nq], op=MUL),
                                          reads=[evk, 'TAB'], writes=[evk])
                                else:
                                    tv = TAB[:, idx * 64: idx * 64 + 64]
                                    win = bass.AP(tv.tensor, tv.offset, [list(tv.ap[0]), [64, nb_], [1, nq]])
                                    Sc.op('dve', I('tensor_tensor', out=ev[:, 0:nb_ * nq].rearrange("p (s c) -> p s c", c=nq),
                                                   in0=ev[:, 0:nb_ * nq].rearrange("p (s c) -> p s c", c=nq), in1=win, op=MUL),
                                          reads=[evk, 'TAB'], writes=[evk])
                            fns = []
                            for i_, kr in enumerate(rows):
                                first = (sdone == 0)
                                lastk = (sdone == nsteps - 1)
                                sdone += 1
                                fns += [I('matmul', O_[0:64, 0:nq], lhsT=V2v[0:64, kr, :], rhs=ev[0:64, i_ * nq:(i_ + 1) * nq], start=first, stop=lastk),
                                        I('matmul', O_[64:128, 0:nq], lhsT=V2v[64:128, kr, :], rhs=ev[64:128, i_ * nq:(i_ + 1) * nq], start=first, stop=lastk)]
                            Sc.op('pe', fns, reads=[evk] + [('V', s, g_, q) for g_ in {kr // 8 for kr in rows} for q in range(4)], writes=[Ok_])
                            wdt = nb_ * nq
                            if bi == 0:
                                Sc.op('pool', I('tensor_copy', out=esum[:, 0:wdt], in_=ev[:, 0:wdt]), reads=[evk], writes=[esumk])
                                w0 = wdt
                            else:
                                Sc.op('pool', I('tensor_tensor', out=esum[:, 0:wdt], in0=esum[:, 0:wdt], in1=ev[:, 0:wdt], op=ADD), reads=[evk, esumk], writes=[esumk])
                        nsl = w0 // nq
                        Sc.op('pe', [I('matmul', Sm[:, 0:nq], lhsT=bd64_f, rhs=esum[:, i_ * nq:(i_ + 1) * nq], start=(i_ == 0), stop=(i_ == nsl - 1)) for i_ in range(nsl)],
                              reads=[esumk, 'CONS'], writes=[Smk])
                        tmpf.release(esumk)
                        rr, rrk = tmpf.get()
                        if _CFIN:
                            Sc.op('dve', I('reciprocal', out=rr[:, 0:nq], in_=Sm[:, 0:nq]), reads=[Smk], writes=[rrk])
                        else:
                            Sc.op('act', I('activation', out=rr[:, 0:nq], in_=Sm[:, 0:nq], func=AF.Ln), reads=[Smk], writes=[rrk])
                            Sc.op('act', I('activation', out=rr[:, 0:nq], in_=rr[:, 0:nq], func=AF.Exp, scale=-1.0), reads=[rrk], writes=[rrk])
                        so = (r0 % 8) * 64
                        Sc.op('dve', I('tensor_tensor', out=stage[:, so:so + nq], in0=O_[:, 0:nq], in1=rr[:, 0:nq], op=MUL), reads=[Ok_, rrk], writes=[stagek])
                        psr.release(Ok_)
                        psr.release(Smk)
                        if (r0 + nr) % 8 == 0 or r0 == 32:
                            gq0, gn = GROUPS[qg]
                            Sc.dma('sp', ('yst', stagek[1]), I('dma_start', out=ysc_d[1, :, j, gq0:gq0 + gn], in_=stage[:, 0:gn]),
                                   reads=[stagek], writes=[('ysc', 1, j, qg)])
                            tmpb.release(stagek)
                dump("yc%d" % l, ysc_d[1], [128, 4, TT], BF16, [('ysc', 1, h, g) for h in range(4) for g in range(ngrp)])
                if stop_after == "yc":
                    Sc.fence()
                    done()
                    return nc, list(dbg_out.keys())

                Sc.fence()
                wring.held.discard(5)
                Sc.dma('sp', 'c4', I('dma_start', out=VNG, in_=vng_d[l].partition_broadcast(128)), writes=['VNG'])
                Sc.dma('sp', 'c5', I('dma_start', out=DBS, in_=dbs_d[l].partition_broadcast(128)), writes=['DBS'])
                Sc.dma('pool', 'c7', I('dma_start', out=WPOOL[:], in_=pool_w_d[l].rearrange("g c d -> c g d")), writes=['WPOOL'])
                Sc.dma('pool', 'c8', I('dma_start', out=WST[:], in_=wsT_d[l].rearrange("g q p -> q g p")), writes=['WST'])
                gate_bc(l, 16, 0, b)
                if not last:
                    gate_bc(l, 16, 1, nb)
                for g in range(ngrp):
                    t0, n = GROUPS[g]
                    nsub = n // 128
                    isctx = (g == 4)
                    slot = 1 if isctx else 0
                    seq0, seq1 = (2048, 2304) if isctx else (0, 2048)
                    has_l = t0 > seq0
                    has_r = t0 + n < seq1
                    Sc.dma('sp', 'yt0', I('dma_start', out=YT[:, 0, :, 0:n], in_=ysc_d[0, :, :, t0:t0 + n]),
                           reads=[('ysc', 0, h, g) for h in range(4)], writes=[('YT', 0)])
                    Sc.dma('sp', 'yt1', I('dma_start', out=YT[:, 1, :, 0:n], in_=ysc_d[1, :, :, t0:t0 + n]),
                           reads=[('ysc', 1, h, g) for h in range(4)], writes=[('YT', 1)])
                    for pg in range(4):
                        w = POOL_W[pg]
                        wB, wBk = wload_t(l, 40 + pg, 8, 128)
                        ps, psk = proj_fm(wB, wBk, g)
                        ph, phk = None, None
                        if has_l or has_r:
                            ph, phk = psr.get()
                            fns = []
                            rd = [wBk]
                            if has_l:
                                fns += [I('matmul', ph[:, 0:8], lhsT=wB[:, k, :], rhs=HT[:, k, t0 - 8:t0], start=(k == 0), stop=(k == 7)) for k in range(8)]
                                rd += ht_keys(g - 1)
                            if has_r:
                                fns += [I('matmul', ph[:, 8:16], lhsT=wB[:, k, :], rhs=HT[:, k, t0 + n:t0 + n + 8], start=(k == 0), stop=(k == 7)) for k in range(8)]
                                rd += ht_keys(g + 1)
                            Sc.op('pe', fns, reads=rd, writes=[phk])
                        pl, plk = tmpb.get()
                        for half in range(n // 256):
                            c0 = half * 256
                            pp, ppk = tmpf.get()
                            Sc.op('act', I('activation', out=pp[:, 8:264], in_=ps[:, c0:c0 + 256], func=AF.Copy), reads=[psk], writes=[ppk])
                            if half > 0:
                                Sc.op('dve', I('tensor_copy', out=pp[:, 0:8], in_=ps[:, c0 - 8:c0]), reads=[psk, ppk], writes=[ppk])
                            elif has_l:
                                Sc.op('dve', I('tensor_copy', out=pp[:, 0:8], in_=ph[:, 0:8]), reads=[phk, ppk], writes=[ppk])
                            else:
                                Sc.op('dve', I('memset', pp[:, 0:8], 0.0), reads=[ppk], writes=[ppk])
                            if c0 + 256 < n:
                                Sc.op('dve', I('tensor_copy', out=pp[:, 264:272], in_=ps[:, c0 + 256:c0 + 264]), reads=[psk, ppk], writes=[ppk])
                            elif has_r:
                                Sc.op('dve', I('tensor_copy', out=pp[:, 264:272], in_=ph[:, 8:16]), reads=[phk, ppk], writes=[ppk])
                            else:
                                Sc.op('dve', I('memset', pp[:, 264:272], 0.0), reads=[ppk], writes=[ppk])
                            b1, b1k = tmpf.get()
                            Sc.op('dve', I('tensor_tensor', out=b1[:, 1:272], in0=pp[:, 0:271], in1=pp[:, 1:272], op=ADD), reads=[ppk], writes=[b1k])
                            cur, curk = b1, b1k
                            if w >= 4:
                                b2, b2k = tmpf.get()
                                Sc.op('dve', I('tensor_tensor', out=b2[:, 2:271], in0=b1[:, 1:270], in1=b1[:, 3:272], op=ADD), reads=[b1k], writes=[b2k])
                                cur, curk = b2, b2k
                            if w >= 8:
                                Sc.op('dve', I('tensor_tensor', out=b1[:, 4:269], in0=b2[:, 2:267], in1=b2[:, 6:271], op=ADD), reads=[b2k], writes=[b1k])
                                cur, curk = b1, b1k
                            if w >= 16:
                                Sc.op('dve', I('tensor_tensor', out=b2[:, 8:264], in0=b1[:, 4:260], in1=b1[:, 12:268], op=ADD), reads=[b1k], writes=[b2k])
                                cur, curk = b2, b2k
                            if half == 0 and not has_l:
                                cc = C_CORR + pg * 16
                                Sc.op('dve', I('tensor_tensor', out=cur[:, 8:16], in0=cur[:, 8:16], in1=CONS[:, cc:cc + 8], op=MUL), reads=[curk, 'CONS'], writes=[curk])
                            if c0 + 256 == n and not has_r:
                                cc = C_CORR + pg * 16 + 8
                                Sc.op('dve', I('tensor_tensor', out=cur[:, 256:264], in0=cur[:, 256:264], in1=CONS[:, cc:cc + 8], op=MUL), reads=[curk, 'CONS'], writes=[curk])
                            Sc.op('dve', I('scalar_tensor_tensor', out=pl[:, c0:c0 + 256], in0=cur[:, 8:264], scalar=1.0 / w, in1=pp[:, 8:264], op0=MUL, op1=SUB),
                                  reads=[curk, ppk], writes=[plk])
                        psY, psYk = psr.get()
                        Sc.op('pe', I('matmul', psY[:, 0:n], lhsT=WPOOL[:, pg, :], rhs=pl[:, 0:n], start=True, stop=True), reads=[plk, 'WPOOL'], writes=[psYk])
                        Sc.op('act', I('activation', out=YBD[:, 0, pg, 0:n], in_=psY[:, 0:n], func=AF.Copy, scale=SMALLP[:, l, 69 + pg:70 + pg]),
                              reads=[psYk, 'SMALLP'], writes=[('YB', pg)])
                    wdv = [wload_t(l, 48 + hh, 8, 256) for hh in range(2)]
                    Sc.op('dve', I('memset', SS1[:, 4:8], 0.0), writes=[('SS1d', sub) for sub in range(4)])
                    for sub in range(nsub):
                        ps, psk = psr.get()
                        fns = []
                        for hh in range(2):
                            for k in range(8):
                                fns.append(I('matmul', ps[:, hh * 256:(hh + 1) * 256], lhsT=HT[:, k, t0 + sub * 128:t0 + (sub + 1) * 128],
                                             rhs=wdv[hh][0][:, k, :], start=(k == 0), stop=(k == 7)))
                        Sc.op('pe', fns, reads=[wdv[0][1], wdv[1][1], ('HT', g, sub), ('HTb', g, sub)], writes=[psk])
                        jk, jkk = tmpb.get()
                        Sc.op('act', I('activation', out=jk[:, 0:512], in_=ps[:, 0:512], func=AF.Square, accum_out=SS1[:, 4 + sub:5 + sub]),
                              reads=[psk], writes=[jkk, ('SS1d', sub)])
                        Sc.op('act', I('activation', out=SS1[:, 4 + sub:5 + sub], in_=SS1[:, 4 + sub:5 + sub], func=AF.Sqrt, scale=1.0 / 512, bias=EPS), reads=[('SS1d', sub)], writes=[('SS1d', sub)])
                        Sc.op('dve', I('reciprocal', out=SS1[:, 4 + sub:5 + sub], in_=SS1[:, 4 + sub:5 + sub]), reads=[('SS1d', sub)], writes=[('SS1d', sub)])
                        Sc.op('dve', I('scalar_tensor_tensor', out=VN[:, sub, :], in0=ps[:, 0:512], scalar=SS1[:, 4 + sub:5 + sub], in1=VNG, op0=MUL, op1=MUL),
                              reads=[psk, ('SS1d', sub), 'VNG'], writes=[('VN', sub)])
                    for pg in range(4):
                        wu, wuk = wload_t(l, 44 + pg, 8, 128)
                        psU, psUk = proj_fm(wu, wuk, g)
                        psr.hold(psUk)
                        psV, psVk = psr.get()
                        Sc.op('pe', [I('matmul', psV[:, sub * 128:(sub + 1) * 128], lhsT=VN[:, sub, pg * 128:(pg + 1) * 128], rhs=WST[:, pg, :], start=True, stop=True)
                                     for sub in range(nsub)], reads=[('VN', sub) for sub in range(nsub)] + ['WST'], writes=[psVk])
                        tt_, ttk = tmpf.get()
                        Sc.op('dve', I('tensor_tensor', out=tt_[:, 0:n].rearrange("p (s c) -> p s c", c=128), in0=psV[:, 0:n].rearrange("p (s c) -> p s c", c=128),
                                       in1=DBS[:, pg * 128:(pg + 1) * 128].unsqueeze(1).to_broadcast([128, nsub, 128]), op=ADD),
                              reads=[psVk, 'DBS'], writes=[ttk])
                        Sc.op('dve', I('tensor_tensor', out=YBD[:, 1, pg, 0:n], in0=psU[:, 0:n], in1=tt_[:, 0:n], op=MUL), reads=[psUk, ttk], writes=[('YD', pg)])
                        psr.release(psUk)
                    if g == 0:
                        dump("ybd%d" % l, YBD, [128, 2, 4, 512], BF16, [('YB', pg) for pg in range(4)] + [('YD', pg) for pg in range(4)])
                    ysrc = [YT[:, 0], YBD[:, 0], YT[:, 1], YBD[:, 1]]
                    ykeys = [[('YT', 0)], [('YB', pg) for pg in range(4)], [('YT', 1)], [('YD', pg) for pg in range(4)]]
                    for j in range(8):
                        wb, wbk = wload_t(l, 32 + j, 16, 128)
                        wgs = []
                        for ip in range(2):
                            v_, key_ = wring.get()
                            for hh in range(2):
                                i_ = ip * 2 + hh
                                view = v_[:, hh * 1024:(hh + 1) * 1024].rearrange("p (k c) -> p k c", k=8)
                                src = wsc_d[l, j * 4 + i_, :, 0:1024].rearrange("p (k c) -> p k c", k=8)
                                if hh == 0:
                                    Sc.dma('sp', key_, I('dma_start', out=view, in_=src), writes=[key_, (key_, 1)])
                                    wgs.append((view, key_))
                                else:
                                    Sc.dma('sp', (key_, 'b'), I('dma_start', out=view, in_=src), writes=[(key_, 1)])
                                    wgs.append((view, (key_, 1)))
                        acc, acck = tmpf.get()
                        tmpf.hold(acck)
                        for i in range(4):
                            psZ, psZk = psr.get()
                            Sc.op('pe', [I('matmul', psZ[:, 0:n], lhsT=wb[:, i * 4 + k, :], rhs=ysrc[i][:, k, 0:n], start=(k == 0), stop=(k == 3)) for k in range(4)],
                                  reads=[wbk] + ykeys[i], writes=[psZk])
                            psG, psGk = psr.get()
                            wg, wgk = wgs[i]
                            Sc.op('pe', [I('matmul', psG[:, 0:n], lhsT=wg[:, k, :], rhs=HT[:, k, t0:t0 + n], start=(k == 0), stop=(k == 7)) for k in range(8)],
                                  reads=[wgk] + ht_keys(g), writes=[psGk])
                            sg, sgk = tmpf.get()
                            Sc.op('act', I('activation', out=sg[:, 0:n], in_=psG[:, 0:n], func=AF.Sigmoid), reads=[psGk], writes=[sgk])
                            if i == 0:
                                Sc.op('dve', I('tensor_tensor', out=acc[:, 0:n], in0=psZ[:, 0:n], in1=sg[:, 0:n], op=MUL), reads=[psZk, sgk], writes=[acck])
                            else:
                                Sc.op('dve', I('tensor_tensor', out=sg[:, 0:n], in0=psZ[:, 0:n], in1=sg[:, 0:n], op=MUL), reads=[psZk, sgk], writes=[sgk])
                                if i < 3:
                                    Sc.op('dve', I('tensor_tensor', out=acc[:, 0:n], in0=acc[:, 0:n], in1=sg[:, 0:n], op=ADD), reads=[acck, sgk], writes=[acck])
                                else:
                                    Sc.op('dve', I('tensor_tensor', out=ACCT[:, j, 0:n], in0=acc[:, 0:n], in1=sg[:, 0:n], op=ADD), reads=[acck, sgk], writes=[('ACCT', j)])
                        tmpf.release(acck)
                    for qd in range(4):
                        wo, wok = wload_t(l, 50 + qd, 8, 256)
                        for sub in range(nsub):
                            tt = t0 // 128 + sub
                            psO, psOk = psr.get()
                            Sc.op('pe', [I('matmul', psO[:, 0:256], lhsT=ACCT[:, k, sub * 128:(sub + 1) * 128], rhs=wo[:, k, :], start=(k == 0), stop=(k == 7)) for k in range(8)],
                                  reads=[wok] + [('ACCT', k) for k in range(8)], writes=[psOk])
                            tq, tqk = tmpf.get()
                            Sc.op('dve', I('tensor_tensor', out=tq[:, 0:256], in0=psO[:, 0:256], in1=GBC[:, slot, qd * 256:(qd + 1) * 256], op=MUL),
                                  reads=[psOk, ('GBC', slot)], writes=[tqk])
                            Sc.op('dve', I('tensor_tensor', out=X[:, tt, qd * 256:(qd + 1) * 256], in0=X[:, tt, qd * 256:(qd + 1) * 256], in1=tq[:, 0:256], op=ADD),
                                  reads=[tqk, ('X', tt)], writes=[('X', tt)])
                dump("xa%d" % l, X[:], [128, NTT, D], F32, [('X', tt) for tt in range(NTT)])
                if stop_after == "xa":
                    Sc.fence()
                    done()
                    return nc, list(dbg_out.keys())

                Sc.fence()
                ftts = list(range(16)) if last else list(range(NTT))
                norm_stats(1)
                make_ht(1, 4, ftts)
                gate_bc(l, 40, 0, b)
                if not last:
                    gate_bc(l, 40, 1, nb)
                ACT2, _ = carve(8192, [128, 2, 2, 512], BF16)
                WDS, _ = carve(12288, [128, 2, 2, 1024], BF16)
                for hg in range(FFN // 256):
                    wa, wak = wload(w_gu_d[l, :, hg * 256:(hg + 1) * 256].rearrange("(k p) n -> p k n", p=128), 8, 256)
                    wbb, wbbk = wload(w_gu_d[l, :, FFN + hg * 256:FFN + (hg + 1) * 256].rearrange("(k p) n -> p k n", p=128), 8, 256)
                    wd, wdk = wload(w_down_d[l, hg * 256:(hg + 1) * 256, :].rearrange("(k p) n -> p k n", p=128), 2, 1024)
                    wsb = hg % 2
                    Sc.op('dve', I('tensor_tensor', out=WDS[:, wsb], in0=wd, in1=GBC[:, 0, :].unsqueeze(1).to_broadcast([128, 2, D]), op=MUL),
                          reads=[wdk, ('GBC', 0)], writes=[('WDS', wsb)])
                    for g in range(ngrp):
                        t0, n = GROUPS[g]
                        nsub = n // 128
                        slot = 1 if g == 4 else 0
                        ab = (hg * ngrp + g) % 2
                        for c2 in range(2):
                            psA, psAk = proj_fm(wa, wak, g, col0=c2 * 128)
                            psB, psBk = proj_fm(wbb, wbbk, g, col0=c2 * 128)
                            sa, sak = tmpf.get()
                            Sc.op('act', I('activation', out=sa[:, 0:n], in_=psA[:, 0:n], func=AF.Silu), reads=[psAk], writes=[sak])
                            Sc.op('dve', I('tensor_tensor', out=ACT2[:, ab, c2, 0:n], in0=psB[:, 0:n], in1=sa[:, 0:n], op=MUL), reads=[psBk, sak], writes=[('ACT2', ab, c2)])
                        for sub in range(nsub):
                            tt = t0 // 128 + sub
                            for half in range(2):
                                psO, psOk = psr.get()
                                if g < 4:
                                    Sc.op('pe', [I('matmul', psO[:, 0:512], lhsT=ACT2[:, ab, c2, sub * 128:(sub + 1) * 128], rhs=WDS[:, wsb, c2, half * 512:(half + 1) * 512],
                                                   start=(c2 == 0), stop=(c2 == 1)) for c2 in range(2)],
                                          reads=[('WDS', wsb), ('ACT2', ab, 0), ('ACT2', ab, 1)], writes=[psOk])
                                    Sc.op('dve', I('tensor_tensor', out=X[:, tt, half * 512:(half + 1) * 512], in0=psO[:, 0:512], in1=X[:, tt, half * 512:(half + 1) * 512], op=ADD),
                                          reads=[psOk, ('X', tt)], writes=[('X', tt)])
                                    continue
                                Sc.op('pe', [I('matmul', psO[:, 0:512], lhsT=ACT2[:, ab, c2, sub * 128:(sub + 1) * 128], rhs=wd[:, c2, half * 512:(half + 1) * 512],
                                               start=(c2 == 0), stop=(c2 == 1)) for c2 in range(2)],
                                      reads=[wdk, ('ACT2', ab, 0), ('ACT2', ab, 1)], writes=[psOk])
                                tq, tqk = tmpf.get()
                                Sc.op('dve', I('tensor_tensor', out=tq[:, 0:512], in0=psO[:, 0:512], in1=GBC[:, slot, half * 512:(half + 1) * 512], op=MUL),
                                      reads=[psOk, ('GBC', slot)], writes=[tqk])
                                Sc.op('dve', I('tensor_tensor', out=X[:, tt, half * 512:(half + 1) * 512], in0=X[:, tt, half * 512:(half + 1) * 512], in1=tq[:, 0:512], op=ADD),
                                      reads=[tqk, ('X', tt)], writes=[('X', tt)])
                wring.held.add(5)
                dump("xo%d" % l, X[:], [128, NTT, D], F32, [('X', tt) for tt in range(NTT)])
                if stop_after == "xo%d" % l:
                    Sc.fence()
                    done()
                    return nc, list(dbg_out.keys())
                Sc.fence()
            for q in range(4):
                Sc.dma('sp', 'yout%d' % q, I('dma_start', out=y_d[b * S + q * 512: b * S + (q + 1) * 512, :].rearrange("(t p) d -> p t d", p=128),
                                             in_=X[:, q * 4:(q + 1) * 4, :]), reads=[('X', q * 4 + i) for i in range(4)])
        done()
    return nc, list(dbg_out.keys())


def _const_tables():
    c = np.zeros((128, C_END), np.float32)
    c[:, C_IDENT:C_IDENT + 128] = np.eye(128, dtype=np.float32)
    bd = np.zeros((128, 128), np.float32)
    bd[0:64, 0:64] = 1.0
    bd[64:128, 64:128] = 1.0
    c[:, C_BD64:C_BD64 + 128] = bd
    rm = np.zeros((128, 128), np.float32)
    for blk in (0, 64):
        for q in range(4):
            for i in range(16):
                o = blk + q * 16 + i
                if q % 2 == 0:
                    rm[o + 16, o] = -1.0
                else:
                    rm[o - 16, o] = 1.0
    c[:, C_RM:C_RM + 128] = rm
    nf = 16
    inv = (10000.0 ** (-np.arange(nf, dtype=np.float32) / nf)).astype(np.float32)
    rows = np.arange(32, dtype=np.float32)
    cols = np.arange(64, dtype=np.float32)
    for cs, fn in enumerate((np.cos, np.sin)):
        base = C_ROPE + cs * 96
        for p in range(128):
            d = p % 64
            f = d % 16
            if d < 32:
                c[p, base:base + 32] = fn((rows * inv[f]).astype(np.float32))
            else:
                c[p, base + 32:base + 96] = fn((cols * inv[f]).astype(np.float32))
    L = 1 << 20
    for g, w in enumerate(POOL_W):
        for t in range(8):
            lo, hi = max(t - w // 2, 0), t + w // 2
            c[:, C_CORR + g * 16 + t] = w / float(hi - lo)
            te = L - 8 + t
            lo, hi = te - w // 2, min(te + w // 2, L)
            c[:, C_CORR + g * 16 + 8 + t] = w / float(hi - lo)
    sw = np.zeros((128, 128), np.float32)
    for i in range(128):
        sw[(i + 64) % 128, i] = 1.0
    c[:, C_SWAP:C_SWAP + 128] = sw
    return c


def _rpb_layout(c_rpb):
    Ld = c_rpb.shape[0]
    kc = np.arange(64)[:, None]
    qc = np.arange(64)[None, :]
    dc = np.clip(kc - qc + 15, 0, 30)
    win0 = np.clip(qc - 8, 0, 48)
    colvalid = ((kc >= win0) & (kc < win0 + 16)).astype(np.float32)
    drs = [4, 3, 2, 1, 0, -1, -2, -3, -4, -5] + list(range(7, -8, -1))
    rowvalid = [0, 1, 1, 1, 1, 1, 1, 1, 1, 0] + [1] * 15
    tab = np.zeros((Ld, 4, 128, 25, 64), np.float32)
    mask = np.zeros((128, 25, 64), np.float32)
    for t, (dr, rv) in enumerate(zip(drs, rowvalid)):
        if not rv:
            continue
        for hh in range(2):
            mask[hh * 64:(hh + 1) * 64, t, :] = colvalid
            for j in range(4):
                tab[:, j, hh * 64:(hh + 1) * 64, t, :] = c_rpb[:, 2 * j + hh, dr + 7][:, dc]
    return tab.reshape(Ld, 4, 128, NTAB), mask.reshape(128, NTAB)


def prep_shared(inp):
    f = lambda a: np.ascontiguousarray(np.asarray(a, dtype=np.float32))
    Ld = DEPTH
    sp = np.zeros((Ld, 128, NSP), np.float32)
    sp[:, :, 0:48] = f(inp['b_mod']).reshape(Ld, 48, 128).transpose(0, 2, 1)
    sp[:, :, 48:56] = f(inp['norm1_g']).reshape(Ld, 8, 128).transpose(0, 2, 1)
    sp[:, :, 56:64] = f(inp['norm2_g']).reshape(Ld, 8, 128).transpose(0, 2, 1)
    p64 = np.arange(128) % 64
    sp[:, :, 64] = f(inp['a_qk_g'])[:, 0][:, p64]
    sp[:, :, 65] = f(inp['a_qk_g'])[:, 1][:, p64]
    sp[:, :, 66] = f(inp['c_qk_g'])[:, 0][:, p64]
    sp[:, :, 67] = f(inp['c_qk_g'])[:, 1][:, p64]
    sp[:, :, 68] = f(inp['a_subln_g'])
    sp[:, :, 69:73] = f(inp['b_pool_s']).reshape(Ld, 4, 128).transpose(0, 2, 1)
    tab, mask = _rpb_layout(f(inp['c_rpb']))
    return {
        "w_mod": f(inp['w_mod']), "w_in": f(inp['w_in']), "w_branch": f(inp['w_branch']), "w_out": f(inp['w_out']),
        "w_gu": f(inp['w_gu']), "w_down": f(inp['w_down']), "b_pool_w": f(inp['b_pool_w']),
        "d_wsT": np.ascontiguousarray(f(inp['d_ws']).transpose(0, 1, 3, 2)),
        "smallp": sp, "vn_g": f(inp['d_vn_g']), "d_bs": f(inp['d_bs']).reshape(Ld, 512),
        "a_lambda": f(inp['a_lambda']).reshape(Ld, 256), "rpbtab": tab, "rpbmask": mask, "consts": _const_tables(),
    }


def prep_core(inp, shared, b0, nb):
    f = lambda a: np.ascontiguousarray(np.asarray(a, dtype=np.float32))
    m = dict(shared)
    m["x"] = f(inp['x'][b0:b0 + nb]).reshape(nb * S, D)
    m["ctx"] = f(inp['ctx'][b0:b0 + nb]).reshape(nb * CTX, D)
    cs = np.concatenate([f(inp['c'][b0:b0 + nb]), f(inp['c_ctx'])[None, :]], axis=0)
    m["cT"] = np.ascontiguousarray(cs.reshape(nb + 1, 8, 128).transpose(2, 1, 0)).reshape(128, 8 * (nb + 1))
    return m


_CACHE = {}


def kernel(**inputs):
    nb = inputs['x'].shape[0] // N_CORES
    if 'nc' not in _CACHE:
        _CACHE['nc'] = build(nb=nb)[0]
    nc = _CACHE['nc']
    shared = prep_shared(inputs)
    in_maps = [prep_core(inputs, shared, c * nb, nb) for c in range(N_CORES)]
    res = run_bass_kernel_spmd(nc, in_maps, core_ids=list(range(N_CORES)))
    out = np.concatenate([np.asarray(r["y"]).reshape(nb, S, D) for r in res.results], axis=0)
    return out.astype(np.float32)
```

```python
import math
from contextlib import ExitStack

import numpy as np
import concourse.bass as bass
import concourse.mybir as mybir
from concourse.bass_utils import run_bass_kernel_spmd

F32 = mybir.dt.float32
BF16 = mybir.dt.bfloat16
AF = mybir.ActivationFunctionType
ALU = mybir.AluOpType

import os
N_CORES = 8
_CDIV = int(os.environ.get('C_DIV', '1'))
_CFIN = int(os.environ.get('C_FIN', '1'))
D = 1024
S = 2048
CTX = 256
TT = S + CTX
NTT = TT // 128
DEPTH = 2
IN_COLS = 8704
OFF_AQ, OFF_AK, OFF_AV, OFF_B, OFF_CQ, OFF_CK, OFF_CV, OFF_DU, OFF_DV, OFF_G = (
    0, 512, 1024, 1536, 2048, 2560, 3072, 3584, 4096, 4608)
FFN = 2816
EPS = 1e-6
POOL_W = (2, 4, 8, 16)
NSP = 80
C_IDENT, C_BD64, C_RM, C_ROPE, C_CORR, C_SWAP, C_END = 0, 128, 256, 384, 576, 640, 768
NTAB = 25 * 64


class _Eng:
    def __init__(self, name, sem, self_sync=True):
        self.name = name
        self.sem = sem
        self.count = 0
        self.clock = {}
        self.prog = []
        self.self_sync = self_sync


class Sched:
    def __init__(self, sems, dma_sems):
        self.eng = {
            'pe': _Eng('pe', sems['pe'], self_sync=False),
            'act': _Eng('act', sems['act']),
            'dve': _Eng('dve', sems['dve']),
            'pool': _Eng('pool', sems['pool']),
            'sp': _Eng('sp', None),
        }
        self.dma_free = list(dma_sems)
        self.dma_key = {}
        self.last_w = {}
        self.readers = {}
        self.n_wait = 0
        self.n_ops = 0

    def _need(self, e, tok):
        sem, sname, val, clk = tok
        if e.clock.get(sname, 0) >= val:
            return
        if sem is e.sem and not e.self_sync:
            return
        e.prog.append(('wait', sem, val))
        self.n_wait += 1
        newc = dict(e.clock)
        for k, v in clk.items():
            if newc.get(k, 0) < v:
                newc[k] = v
        if newc.get(sname, 0) < val:
            newc[sname] = val
        e.clock = newc

    def _deps(self, e, reads, writes):
        for r in reads:
            t = self.last_w.get(r)
            if t is not None:
                self._need(e, t)
        for w in writes:
            t = self.last_w.get(w)
            if t is not None:
                self._need(e, t)
            for t in self.readers.get(w, ()):
                self._need(e, t)

    def _commit(self, tok, reads, writes):
        for r in reads:
            self.readers.setdefault(r, []).append(tok)
        for w in writes:
            self.last_w[w] = tok
            self.readers[w] = []

    def op(self, eng, fns, reads=(), writes=()):
        e = self.eng[eng]
        if isinstance(fns, tuple):
            fns = [fns]
        self._deps(e, reads, writes)
        e.count += 1
        for f in fns[:-1]:
            e.prog.append(('ins', f, None))
        e.prog.append(('ins', fns[-1], (e.sem, 1)))
        tok = (e.sem, e.name, e.count, e.clock)
        e.last_tok = tok
        self._commit(tok, reads, writes)
        self.n_ops += len(fns)
        return tok

    def dma(self, queue, key, fn, reads=(), writes=()):
        e = self.eng[queue]
        if key not in self.dma_key:
            self.dma_key[key] = [self.dma_free.pop(), 0, None]
        ent = self.dma_key[key]
        if ent[2] is not None:
            self._need(e, ent[2])
        self._deps(e, reads, writes)
        ent[1] += 16
        sname = 'dma_%s' % (key,)
        e.prog.append(('ins', fn, (ent[0], 16)))
        tok = (ent[0], sname, ent[1], e.clock)
        ent[2] = tok
        self._commit(tok, reads, writes)
        self.n_ops += 1
        return tok

    def fence(self):
        toks = [e.last_tok for e in self.eng.values() if getattr(e, 'last_tok', None) is not None]
        toks += [v[2] for v in self.dma_key.values() if v[2] is not None]
        for e in self.eng.values():
            for t in toks:
                self._need(e, t)

    def wait_all(self, eng, toks):
        e = self.eng[eng]
        for t in toks:
            if t is not None:
                self._need(e, t)

    def emit(self, block):
        def run(e, h):
            for it in e.prog:
                if it[0] == 'wait':
                    h.wait_ge(it[1], it[2])
                else:
                    nm, a, kw = it[1]
                    ins = getattr(h, nm)(*a, **kw)
                    if it[2] is not None:
                        ins.then_inc(it[2][0], it[2][1])

        @block.tensor
        def _(h):
            run(self.eng['pe'], h)

        @block.scalar
        def _(h):
            run(self.eng['act'], h)

        @block.vector
        def _(h):
            run(self.eng['dve'], h)

        @block.gpsimd
        def _(h):
            run(self.eng['pool'], h)

        @block.sync
        def _(h):
            run(self.eng['sp'], h)


class _Rot:
    def __init__(self, name, views):
        self.name = name
        self.views = views
        self.i = 0
        self.held = set()

    def get(self):
        n = len(self.views)
        for _ in range(n):
            i = self.i
            self.i = (self.i + 1) % n
            if i not in self.held:
                return self.views[i], (self.name, i)
        raise RuntimeError("rot pool exhausted " + self.name)

    def hold(self, key):
        self.held.add(key[1])

    def release(self, key):
        self.held.discard(key[1])


def I(name, *a, **kw):
    return (name, a, kw)


def build(nb=4, depth=DEPTH, dbg=(), stop_after=None):
    nc = bass.Bass("TRN2", target_bir_lowering=False)
    dbg = set(dbg)
    NJ = nb + 1

    def din(name, shape, dt=F32):
        return nc.dram_tensor(name, list(shape), dt, kind="ExternalInput").ap()

    x_d = din("x", [nb * S, D])
    ctx_d = din("ctx", [nb * CTX, D])
    cT_d = din("cT", [128, 8 * NJ])
    w_mod_d = din("w_mod", [DEPTH, D, 6 * D])
    w_in_d = din("w_in", [DEPTH, D, IN_COLS])
    w_branch_d = din("w_branch", [DEPTH, 2048, D])
    w_out_d = din("w_out", [DEPTH, D, D])
    w_gu_d = din("w_gu", [DEPTH, D, 2 * FFN])
    w_down_d = din("w_down", [DEPTH, FFN, D])
    pool_w_d = din("b_pool_w", [DEPTH, 4, 128, 128])
    wsT_d = din("d_wsT", [DEPTH, 4, 128, 128])
    smallp_d = din("smallp", [DEPTH, 128, NSP])
    vng_d = din("vn_g", [DEPTH, 512])
    dbs_d = din("d_bs", [DEPTH, 512])
    alam_d = din("a_lambda", [DEPTH, 256])
    rpbtab_d = din("rpbtab", [DEPTH, 4, 128, NTAB])
    rpbmask_d = din("rpbmask", [128, NTAB])
    consts_d = din("consts", [128, C_END])
    y_d = nc.dram_tensor("y", [nb * S, D], F32, kind="ExternalOutput").ap()
    ysc_d = nc.dram_tensor("ysc", [2, 128, 4, TT], BF16).ap()
    NWT = 54
    wsc_d = nc.dram_tensor("wsc", [DEPTH, NWT, 128, 2048], BF16).ap()
    dbg_out = {}

    es = ExitStack()
    with es:
        def sb(name, shape, dt):
            return es.enter_context(nc.sbuf_tensor(name, list(shape), dt))

        X = sb("X", [128, NTT, D], F32)
        HT = sb("HT", [128, 8, TT], BF16)
        WR = sb("WR", [128, 5, 2048], BF16)
        TMPF = sb("TMPF", [128, 6, 512], F32)
        TMPB = sb("TMPB", [128, 4, 512], BF16)
        XH = sb("XH", [128, 2, D], BF16)
        ARENA = sb("ARENA", [128, 11392], F32)
        CONS = sb("CONS", [128, C_END], F32)
        IDB = sb("IDB", [128, 128], BF16)
        ONEB = sb("ONEB", [128, 128], BF16)
        BD64B = sb("BD64B", [128, 128], BF16)
        SWAPB = sb("SWAPB", [128, 128], BF16)
        ONEF = sb("ONEF", [128, 128], F32)
        DIAG = sb("DIAG", [128, 2, 128], F32)
        SMALLP = sb("SMALLP", [128, DEPTH, NSP], F32)
        MODT = sb("MODT", [128, DEPTH, 48, NJ], F32)
        SCT = sb("SCT", [128, 8, NJ], BF16)
        CTF = sb("CTF", [128, 8 * NJ], F32)
        MV = sb("MV", [128, 8, 8], F32)
        SS = sb("SS", [128, 2, NTT], F32)
        SS1 = sb("SS1", [128, 8], F32)
        LAM = sb("LAM", [128, DEPTH, 4], F32)
        ALAM = sb("ALAM", [128, 256], F32)
        WPOOL = sb("WPOOL", [128, 4, 128], BF16)
        WST = sb("WST", [128, 4, 128], BF16)
        JUNK = sb("JUNK", [128, D], BF16)

        def carve(off_bytes, shape, dt):
            n = 1
            for d_ in shape[1:]:
                n *= d_
            nbytes = n * (2 if dt == BF16 else 4)
            assert off_bytes % 4 == 0 and off_bytes + nbytes <= 11392 * 4, (off_bytes, shape)
            v = ARENA[:, off_bytes // 4:(off_bytes + nbytes + 3) // 4]
            if dt == BF16:
                v = v.bitcast(BF16)
            if len(shape) == 3:
                v = v.rearrange("p (a b) -> p a b", a=shape[1])
            elif len(shape) == 4:
                v = v.rearrange("p (a b c) -> p a b c", a=shape[1], b=shape[2])
            return v, off_bytes + nbytes

        QKV, o_ = carve(0, [128, 2, 3, TT], BF16)
        TAB, o_ = carve(o_, [128, NTAB], BF16)
        MASK, o_ = carve(o_, [128, NTAB], BF16)
        ROPE, o_ = carve(o_, [128, 2, 512], F32)
        QZ, o_ = carve(o_, [128, 2, 2, 512], BF16)
        GBC, o_ = carve(0, [128, 2, D], F32)
        YT, o_ = carve(o_, [128, 2, 4, 512], BF16)
        YBD, o_ = carve(o_, [128, 2, 4, 512], BF16)
        ACCT, o_ = carve(o_, [128, 8, 512], BF16)
        WX, o_ = carve(o_, [128, 2048], BF16)
        VN, o_ = carve(o_, [128, 4, 512], BF16)
        VNG, o_ = carve(o_, [128, 512], F32)
        DBS, o_ = carve(o_, [128, 512], F32)
        ACTT, _o2 = carve(8192, [128, 2, 512], BF16)

        PS = [es.enter_context(nc.psum_tensor("PS%d" % i, [128, 512], F32)) for i in range(8)]
        sems = {k: es.enter_context(nc.semaphore("sem_" + k)) for k in ['pe', 'act', 'dve', 'pool']}
        dsems = [es.enter_context(nc.semaphore("dsem%d" % i)) for i in range(72)]
        block = es.enter_context(nc.Block())
        Sc = Sched(sems, dsems)

        psr = _Rot('ps', [p[:] for p in PS])
        tmpf = _Rot('tmpf', [TMPF[:, i, :] for i in range(6)])
        tmpb = _Rot('tmpb', [TMPB[:, i, :] for i in range(4)])
        wring = _Rot('W', [WR[:, i, :] for i in range(5)] + [WX])
        wring.held.add(5)
        xh = _Rot('xh', [XH[:, i, :] for i in range(2)])
        diag = _Rot('diag', [DIAG[:, i, :] for i in range(2)])

        ident_f = CONS[:, C_IDENT:C_IDENT + 128]
        bd64_f = CONS[:, C_BD64:C_BD64 + 128]
        rm_f = CONS[:, C_RM:C_RM + 128]
        MUL, ADD, SUB = ALU.mult, ALU.add, ALU.subtract

        def wload(src, a, b):
            v, key = wring.get()
            view = v[:, 0:a * b].rearrange("p (a b) -> p a b", a=a)
            Sc.dma('pool', key, I('dma_start', out=view, in_=src), writes=[key, (key, 1)])
            return view, key

        def wload_t(l, t, a, b):
            v, key = wring.get()
            view = v[:, 0:a * b].rearrange("p (a b) -> p a b", a=a)
            Sc.dma('sp', key, I('dma_start', out=view, in_=wsc_d[l, t, :, 0:a * b].rearrange("p (a b) -> p a b", a=a)), writes=[key, (key, 1)])
            return view, key

        def win_cols(l, c0, n):
            return w_in_d[l, :, c0:c0 + n].rearrange("(k p) n -> p k n", p=128)

        def dump(name, ap_sb, shape, dt, reads):
            if name not in dbg:
                return
            t = nc.dram_tensor("dbg_" + name, list(shape), dt, kind="ExternalOutput").ap()
            dbg_out[name] = Sc.dma('sp', 'dbg_' + name, I('dma_start', out=t, in_=ap_sb), reads=reads)

        Sc.dma('sp', 'c0', I('dma_start', out=CONS[:], in_=consts_d), writes=['CONS'])
        Sc.dma('sp', 'c1', I('dma_start', out=SMALLP[:], in_=smallp_d.rearrange("l p n -> p l n")), writes=['SMALLP'])
        Sc.dma('sp', 'c2', I('dma_start', out=CTF[:], in_=cT_d), writes=['CTF'])
        Sc.op('dve', I('tensor_copy', out=IDB[:], in_=ident_f), reads=['CONS'], writes=['IDB'])
        Sc.op('dve', I('tensor_copy', out=BD64B[:], in_=bd64_f), reads=['CONS'], writes=['BD64B'])
        Sc.op('dve', I('tensor_copy', out=SWAPB[:], in_=CONS[:, C_SWAP:C_SWAP + 128]), reads=['CONS'], writes=['SWAPB'])
        Sc.op('dve', I('memset', ONEB[:], 1.0), writes=['ONEB'])
        Sc.op('dve', I('memset', ONEF[:], 1.0), writes=['ONEF'])
        Sc.op('act', I('activation', out=SCT[:].rearrange("p k j -> p (k j)"), in_=CTF[:], func=AF.Silu), reads=['CTF'], writes=['SCT'])

        pci = [0]

        def precast(l, t, src, a, b):
            Sc.dma('pool', ('pc', pci[0] % 8), I('dma_start', out=wsc_d[l, t, :, 0:a * b].rearrange("p (a b) -> p a b", a=a), in_=src))
            pci[0] += 1
        for l in range(depth):
            for j in range(8):
                for i_ in range(4):
                    precast(l, j * 4 + i_, win_cols(l, OFF_G + i_ * 1024 + j * 128, 128), 8, 128)
                precast(l, 32 + j, w_branch_d[l, :, j * 128:(j + 1) * 128].rearrange("(k p) n -> p k n", p=128), 16, 128)
            for pg in range(4):
                precast(l, 40 + pg, win_cols(l, OFF_B + pg * 128, 128), 8, 128)
                precast(l, 44 + pg, win_cols(l, OFF_DU + pg * 128, 128), 8, 128)
                precast(l, 50 + pg, w_out_d[l, :, pg * 256:(pg + 1) * 256].rearrange("(k p) n -> p k n", p=128), 8, 256)
            for hh in range(2):
                precast(l, 48 + hh, win_cols(l, OFF_DV + hh * 256, 256), 8, 256)

        for l in range(depth):
            lam_init = 0.8 - 0.6 * math.exp(-0.3 * l)
            Sc.dma('sp', 'c6', I('dma_start', out=ALAM[:], in_=alam_d[l].partition_broadcast(128)), writes=['ALAM'])
            Sc.op('dve', I('tensor_tensor', out=ALAM[:, 0:64], in0=ALAM[:, 0:64], in1=ALAM[:, 64:128], op=MUL), reads=['ALAM'], writes=['ALAM'])
            Sc.op('dve', I('tensor_tensor', out=ALAM[:, 128:192], in0=ALAM[:, 128:192], in1=ALAM[:, 192:256], op=MUL), reads=['ALAM'], writes=['ALAM'])
            Sc.op('dve', I('reduce_sum', out=SS1[:, 0:1], in_=ALAM[:, 0:64], axis=mybir.AxisListType.X), reads=['ALAM'], writes=['SS1'])
            Sc.op('dve', I('reduce_sum', out=SS1[:, 1:2], in_=ALAM[:, 128:192], axis=mybir.AxisListType.X), reads=['ALAM', 'SS1'], writes=['SS1'])
            Sc.op('act', I('activation', out=SS1[:, 2:4], in_=SS1[:, 0:2], func=AF.Exp), reads=['SS1'], writes=['SS1'])
            Sc.op('dve', I('tensor_tensor', out=LAM[:, l, 0:1], in0=SS1[:, 2:3], in1=SS1[:, 3:4], op=SUB), reads=['SS1'], writes=[('LAM', l)])
            Sc.op('dve', I('tensor_scalar', out=LAM[:, l, 0:1], in0=LAM[:, l, 0:1], scalar1=lam_init, scalar2=None, op0=ADD), reads=[('LAM', l)], writes=[('LAM', l)])
            Sc.op('dve', I('tensor_scalar', out=LAM[:, l, 1:2], in0=LAM[:, l, 0:1], scalar1=-1.0, scalar2=None, op0=MUL), reads=[('LAM', l)], writes=[('LAM', l)])
            Sc.op('dve', I('tensor_scalar', out=LAM[:, l, 2:3], in0=SMALLP[:, l, 68:69], scalar1=1.0 - lam_init, scalar2=None, op0=MUL), reads=['SMALLP', ('LAM', l)], writes=[('LAM', l)])

        for l in range(depth):
            pm, pmk = psr.get()
            for blk in range(24):
                wv, wk = wload(w_mod_d[l, :, blk * 256:(blk + 1) * 256].rearrange("(k p) n -> p k n", p=128), 8, 256)
                fns = []
                for m in range(2):
                    ch = blk * 2 + m
                    for k in range(8):
                        fns.append(I('matmul', pm[:, ch * NJ:(ch + 1) * NJ], lhsT=wv[:, k, m * 128:(m + 1) * 128], rhs=SCT[:, k, :],
                                     start=(k == 0), stop=(k == 7)))
                Sc.op('pe', fns, reads=[wk, 'SCT'], writes=[pmk])
            Sc.op('dve', I('tensor_tensor', out=MODT[:, l, :, :], in0=pm[:, 0:48 * NJ].rearrange("p (c j) -> p c j", j=NJ),
                           in1=SMALLP[:, l, 0:48].unsqueeze(2).to_broadcast([128, 48, NJ]), op=ADD),
                  reads=[pmk, 'SMALLP'], writes=[('MODT', l)])
        dump("modT", MODT[:], [128, DEPTH, 48, NJ], F32, [('MODT', l) for l in range(depth)])
        Sc.fence()

        GROUPS = [(0, 512), (512, 512), (1024, 512), (1536, 512), (2048, 256)]

        def ht_keys(g):
            return [(nm, g, i) for i in range(4 if g < 4 else 2) for nm in ('HT', 'HTb')]

        def all_ht_keys():
            return [k for g in range(5) for k in ht_keys(g)]

        def norm_stats(which):
            Sc.op('dve', I('memset', SS[:, which, :], 0.0), writes=[('SS', which)])
            for tt in range(NTT):
                Sc.op('act', I('activation', out=JUNK[:], in_=X[:, tt, :], func=AF.Square, accum_out=SS[:, which, tt:tt + 1]),
                      reads=[('X', tt)], writes=['JUNK', ('SS', which)])
            Sc.op('act', I('activation', out=SS[:, which, :], in_=SS[:, which, :], func=AF.Sqrt, scale=1.0 / D, bias=EPS), reads=[('SS', which)], writes=[('SS', which)])
            Sc.op('dve', I('reciprocal', out=SS[:, which, :], in_=SS[:, which, :]), reads=[('SS', which)], writes=[('SS', which)])

        def ffn_ht(g):
            t0, n = GROUPS[g]
            tts = list(range(t0 // 128, (t0 + n) // 128))
            a_, b_ = tts[0], tts[-1] + 1
            Sc.op('dve', I('memset', SS[:, 1, a_:b_], 0.0), writes=[('SS2', g)])
            for tt in tts:
                Sc.op('act', I('activation', out=JUNK[:], in_=X[:, tt, :], func=AF.Square, accum_out=SS[:, 1, tt:tt + 1]),
                      reads=[('X', tt), ('SS2', g)], writes=['JUNK', ('SS2', g)])
            Sc.op('act', I('activation', out=SS[:, 1, a_:b_], in_=SS[:, 1, a_:b_], func=AF.Sqrt, scale=1.0 / D, bias=EPS), reads=[('SS2', g)], writes=[('SS2', g)])
            Sc.op('dve', I('reciprocal', out=SS[:, 1, a_:b_], in_=SS[:, 1, a_:b_]), reads=[('SS2', g)], writes=[('SS2', g)])
            make_ht(1, 4, tts, sskey=('SS2', g))

        def make_ht(which, mva, tts, sskey=None):
            for tt in tts:
                a_i = mva + 2 if tt >= 16 else mva
                xv, xk = xh.get()
                Sc.op('dve', I('tensor_scalar', out=xv, in0=X[:, tt, :], scalar1=SS[:, which, tt:tt + 1], scalar2=None, op0=MUL),
                      reads=[('X', tt), sskey or ('SS', which)], writes=[xk])
                pt, ptk = psr.get()
                ptb = pt.bitcast(BF16)
                Sc.op('pe', [I('transpose', out=ptb[:, k * 128:(k + 1) * 128], in_=xv[:, k * 128:(k + 1) * 128], identity=IDB[:]) for k in range(8)],
                      reads=[xk, 'IDB'], writes=[ptk])
                g = min(tt // 4, 4)
                if tt % 2 == 0:
                    Sc.op('act', [I('activation', out=HT[:, k, tt * 128:(tt + 1) * 128], in_=ptb[:, k * 128:(k + 1) * 128], func=AF.Identity,
                                    scale=MV[:, a_i, k:k + 1], bias=MV[:, a_i + 1, k:k + 1]) for k in range(8)],
                          reads=[ptk, 'MV'], writes=[('HT', g, tt % 4)])
                else:
                    Sc.op('dve', [I('tensor_scalar', out=HT[:, k, tt * 128:(tt + 1) * 128], in0=ptb[:, k * 128:(k + 1) * 128],
                                    scalar1=MV[:, a_i, k:k + 1], scalar2=MV[:, a_i + 1, k:k + 1], op0=MUL, op1=ADD) for k in range(8)],
                          reads=[ptk, 'MV'], writes=[('HT', g, tt % 4)])

        def gate_bc(l, mod_chunk0, slot, jcol):
            for half in range(2):
                pg_, pgk = psr.get()
                for kk in range(4):
                    k = half * 4 + kk
                    dv, dk = diag.get()
                    Sc.op('dve', I('tensor_scalar', out=dv, in0=ident_f, scalar1=MODT[:, l, mod_chunk0 + k, jcol:jcol + 1], scalar2=None, op0=MUL),
                          reads=['CONS', ('MODT', l)], writes=[dk])
                    Sc.op('pe', I('matmul', pg_[:, kk * 128:(kk + 1) * 128], lhsT=ONEF[:], rhs=dv, start=True, stop=True),
                          reads=[dk, 'ONEF'], writes=[pgk])
                Sc.op('act', I('activation', out=GBC[:, slot, half * 512:(half + 1) * 512], in_=pg_, func=AF.Copy),
                      reads=[pgk], writes=[('GBC', slot)])

        def qknorm_gen(ps_raw, psk, n, gcol, rope, out_ap, out_keys):
            sq, sqk = tmpf.get()
            Sc.op('act', I('activation', out=sq[:, 0:n], in_=ps_raw[:, 0:n], func=AF.Square), reads=[psk], writes=[sqk])
            yield
            p2, p2k = psr.get()
            Sc.op('pe', I('matmul', p2[:, 0:n], lhsT=bd64_f, rhs=sq[:, 0:n], start=True, stop=True), reads=[sqk, 'CONS'], writes=[p2k])
            yield
            rs, rsk = tmpf.get()
            Sc.op('act', I('activation', out=rs[:, 0:n], in_=p2[:, 0:n], func=AF.Sqrt, scale=1.0 / 64, bias=EPS), reads=[p2k], writes=[rsk])
            yield
            Sc.op('dve', I('reciprocal', out=rs[:, 0:n], in_=rs[:, 0:n]), reads=[rsk], writes=[rsk])
            yield
            if not rope:
                Sc.op('dve', I('scalar_tensor_tensor', out=out_ap, in0=ps_raw[:, 0:n], scalar=gcol, in1=rs[:, 0:n], op0=MUL, op1=MUL),
                      reads=[psk, rsk, 'SMALLP'], writes=out_keys)
                return
            nn, nk = tmpf.get()
            Sc.op('dve', I('scalar_tensor_tensor', out=nn[:, 0:n], in0=ps_raw[:, 0:n], scalar=gcol, in1=rs[:, 0:n], op0=MUL, op1=MUL),
                  reads=[psk, rsk, 'SMALLP'], writes=[nk])
            yield
            p3, p3k = psr.get()
            Sc.op('pe', I('matmul', p3[:, 0:n], lhsT=rm_f, rhs=nn[:, 0:n], start=True, stop=True), reads=[nk, 'CONS'], writes=[p3k])
            Sc.op('dve', I('tensor_tensor', out=sq[:, 0:n], in0=nn[:, 0:n], in1=ROPE[:, 0, 0:n], op=MUL), reads=[nk, 'ROPE'], writes=[sqk])
            yield
            Sc.op('dve', I('tensor_tensor', out=rs[:, 0:n], in0=p3[:, 0:n], in1=ROPE[:, 1, 0:n], op=MUL), reads=[p3k, 'ROPE'], writes=[rsk])
            yield
            Sc.op('dve', I('tensor_tensor', out=out_ap, in0=sq[:, 0:n], in1=rs[:, 0:n], op=ADD), reads=[sqk, rsk], writes=out_keys)

        def lockstep(gens):
            gens = list(gens)
            while gens:
                nxt_ = []
                for g_ in gens:
                    try:
                        next(g_)
                        nxt_.append(g_)
                    except StopIteration:
                        pass
                gens = nxt_

        def qknorm(*a):
            lockstep([qknorm_gen(*a)])

        def rope_tables(g):
            for cs in range(2):
                base = C_ROPE + cs * 96
                Sc.op('dve', I('tensor_tensor', out=ROPE[:, cs, :].rearrange("p (r c) -> p r c", c=64),
                               in0=CONS[:, base + g * 8: base + g * 8 + 8].unsqueeze(2).to_broadcast([128, 8, 64]),
                               in1=CONS[:, base + 32: base + 96].unsqueeze(1).to_broadcast([128, 8, 64]), op=ADD),
                      reads=['CONS'], writes=['ROPE'])

        def proj_fm(wv, wk, g, col0=0, nk=8, rhs_src=None, rhs_keys=None):
            t0, n = GROUPS[g]
            ps, psk = psr.get()
            Sc.op('pe', [I('matmul', ps[:, 0:n], lhsT=wv[:, k, col0:col0 + 128], rhs=HT[:, k, t0:t0 + n], start=(k == 0), stop=(k == nk - 1))
                         for k in range(nk)], reads=[wk] + ht_keys(g), writes=[psk])
            return ps, psk

        def done():
            Sc.wait_all('sp', list(dbg_out.values()))
            Sc.wait_all('sp', [v[2] for k, v in Sc.dma_key.items() if isinstance(k, str) and k.startswith('yout')])
            Sc.emit(block)
            build.last_sched = Sc
            print("sched: ops", Sc.n_ops, "waits", Sc.n_wait, "dma keys", len(Sc.dma_key),
                  "per-engine", {k: len(v.prog) for k, v in Sc.eng.items()})

        for b in range(nb):
            for q in range(4):
                Sc.dma('sp', 'xin%d' % q, I('dma_start', out=X[:, q * 4:(q + 1) * 4, :],
                                            in_=x_d[b * S + q * 512: b * S + (q + 1) * 512, :].rearrange("(t p) d -> p t d", p=128)),
                       writes=[('X', q * 4 + i) for i in range(4)])
            Sc.dma('sp', 'xin4', I('dma_start', out=X[:, 16:18, :], in_=ctx_d[b * CTX:(b + 1) * CTX, :].rearrange("(t p) d -> p t d", p=128)),
                   writes=[('X', 16), ('X', 17)])
            for l in range(depth):
                last = (l == DEPTH - 1)
                ngrp = 4 if last else 5
                for (row, gcol, sc_ch, sh_ch, jcol) in ((0, 48, 8, 0, b), (2, 48, 8, 0, nb), (4, 56, 32, 24, b), (6, 56, 32, 24, nb)):
                    Sc.op('dve', I('scalar_tensor_tensor', out=MV[:, row, :], in0=MODT[:, l, sc_ch:sc_ch + 8, jcol], scalar=1.0,
                                   in1=SMALLP[:, l, gcol:gcol + 8], op0=ADD, op1=MUL),
                          reads=[('MODT', l), 'SMALLP'], writes=['MV'])
                    Sc.op('dve', I('tensor_copy', out=MV[:, row + 1, :], in_=MODT[:, l, sh_ch:sh_ch + 8, jcol]),
                          reads=[('MODT', l)], writes=['MV'])
                norm_stats(0)
                make_ht(0, 0, range(NTT))
                Sc.fence()
                Sc.dma('pool', 'c3', I('dma_start', out=MASK, in_=rpbmask_d), writes=['MASK'])
                Sc.op('pool', I('memset', QZ[64:128, :, 0, :], 0.0), writes=[('QZ', 0), ('QZ', 1)])
                Sc.op('pool', I('memset', QZ[0:64, :, 1, :], 0.0), writes=[('QZ', 0), ('QZ', 1)])
                qzi = 0
                dump("ht%d" % l, HT[:], [128, 8, TT], BF16, all_ht_keys())
                if stop_after == "ht":
                    done()
                    return nc, list(dbg_out.keys())

                for h in range(4):
                    s = h % 2
                    QA, KA, VA = QKV[:, s, 0, :], QKV[:, s, 1, :], QKV[:, s, 2, :]
                    wq, wqk = wload(win_cols(l, OFF_AQ + h * 128, 128), 8, 128)
                    wk_, wkk = wload(win_cols(l, OFF_AK + h * 128, 128), 8, 128)
                    wv_, wvk = wload(win_cols(l, OFF_AV + h * 128, 128), 8, 128)
                    for g in range(5):
                        t0, n = GROUPS[g]
                        if g < 4:
                            rope_tables(g)
                        gens = []
                        psK, psKk = proj_fm(wk_, wkk, g)
                        gens.append(qknorm_gen(psK, psKk, n, SMALLP[:, l, 65:66], g < 4, KA[:, t0:t0 + n], [('K', s, g)]))
                        if g < ngrp:
                            psQ, psQk = proj_fm(wq, wqk, g)
                            gens.append(qknorm_gen(psQ, psQk, n, SMALLP[:, l, 64:65], g < 4, QA[:, t0:t0 + n], [('Q', s, g)]))
                        ps, psk = psr.get()
                        fns = []
                        for sub in range(n // 128):
                            for k in range(8):
                                fns.append(I('matmul', ps[:, sub * 128:(sub + 1) * 128], lhsT=HT[:, k, t0 + sub * 128: t0 + (sub + 1) * 128],
                                             rhs=wv_[:, k, :], start=(k == 0), stop=(k == 7)))
                        Sc.op('pe', fns, reads=[wvk] + ht_keys(g), writes=[psk])
                        lockstep(gens)
                        Sc.op('act', I('activation', out=VA[:, t0:t0 + n], in_=ps[:, 0:n], func=AF.Copy), reads=[psk], writes=[('V', s, g)])
                    if h == 0:
                        dump("aq%d" % l, QA, [128, TT], BF16, [('Q', s, g) for g in range(ngrp)])
                        dump("ak%d" % l, KA, [128, TT], BF16, [('K', s, g) for g in range(5)])
                        dump("av%d" % l, VA, [128, TT], BF16, [('V', s, g) for g in range(5)])
                        if stop_after == "aqkv":
                            done()
                            return nc, list(dbg_out.keys())
                    for qg in range(ngrp):
                        q0, nq = GROUPS[qg]
                        kcs = list(range(18)) if qg < 4 else [16, 17]
                        steps = [(kc, c) for kc in kcs for c in range(2)]
                        zb = qzi % 2
                        qzi += 1
                        Sc.op('pool', [I('tensor_copy', out=QZ[0:64, zb, 0, 0:nq], in_=QA[0:64, q0:q0 + nq]),
                                       I('tensor_copy', out=QZ[64:128, zb, 1, 0:nq], in_=QA[64:128, q0:q0 + nq])],
                              reads=[('Q', s, qg)], writes=[('QZ', zb)])
                        accs = []
                        for _ in range(2):
                            a_, ak_ = psr.get()
                            psr.hold(ak_)
                            accs.append((a_, ak_))
                        eacc = []
                        for _ in range(2):
                            a_, ak_ = tmpf.get()
                            tmpf.hold(ak_)
                            eacc.append((a_, ak_))
                        s2p, s2pk = psr.get()
                        psr.hold(s2pk)

                        def emit_s(kc, c):
                            pS, pSk = psr.get()
                            Sc.op('pe', I('matmul', pS[:, 0:nq], lhsT=KA[:, kc * 128:(kc + 1) * 128], rhs=QZ[:, zb, c, 0:nq], start=True, stop=True),
                                  reads=[('K', s, kc // 4), ('QZ', zb)], writes=[pSk])
                            return pS, pSk
                        LA = 2
                        pend = [emit_s(*steps[i_]) for i_ in range(min(LA, len(steps)))]
                        for si, (kc, c) in enumerate(steps):
                            pS, pSk = pend.pop(0)
                            if si + LA < len(steps):
                                pend.append(emit_s(*steps[si + LA]))
                            ev, evk = tmpb.get()
                            Sc.op('act', I('activation', out=ev[:, 0:nq], in_=pS[:, 0:nq], func=AF.Exp, scale=0.125), reads=[pSk], writes=[evk])
                            first = (kc == kcs[0])
                            lastk = (kc == kcs[-1])
                            o_, ok_ = accs[c]
                            ea, eak = eacc[c]
                            if c == 0:
                                if first:
                                    Sc.op('pool', I('tensor_copy', out=ea[:, 0:nq], in_=ev[:, 0:nq]), reads=[evk], writes=[eak])
                                else:
                                    Sc.op('pool', I('tensor_tensor', out=ea[:, 0:nq], in0=ea[:, 0:nq], in1=ev[:, 0:nq], op=ADD), reads=[evk, eak], writes=[eak])
                                Sc.op('pe', I('matmul', o_[:, 0:nq], lhsT=VA[:, kc * 128:(kc + 1) * 128], rhs=ev[:, 0:nq], start=first, stop=lastk),
                                      reads=[evk, ('V', s, kc // 4)], writes=[ok_])
                            else:
                                Sc.op('pe', [I('matmul', o_[:, 0:nq], lhsT=VA[:, kc * 128:(kc + 1) * 128], rhs=ev[:, 0:nq], start=first, stop=lastk),
                                             I('matmul', s2p[:, 0:nq], lhsT=ONEB[:], rhs=ev[:, 0:nq], start=first, stop=lastk)],
                                      reads=[evk, ('V', s, kc // 4), 'ONEB'], writes=[ok_, s2pk])
                        (o1, o1k), (o2, o2k) = accs
                        rr = []
                        for c in range(2):
                            ea, eak = eacc[c]
                            if c == 0:
                                p2, p2k = psr.get()
                                Sc.op('pe', I('matmul', p2[:, 0:nq], lhsT=ONEF[:], rhs=ea[:, 0:nq], start=True, stop=True), reads=[eak, 'ONEF'], writes=[p2k])
                            else:
                                p2, p2k = s2p, s2pk
                            Sc.op('dve', I('reciprocal', out=ea[:, 0:nq], in_=p2[:, 0:nq]), reads=[p2k], writes=[eak])
                            rr.append((ea, eak))
                        psr.release(s2pk)
                        (r1, r1k), (r2, r2k) = rr
                        Sc.op('dve', I('tensor_tensor', out=r1[:, 0:nq], in0=o1[:, 0:nq], in1=r1[:, 0:nq], op=MUL), reads=[o1k, r1k], writes=[r1k])
                        Sc.op('dve', I('tensor_tensor', out=r2[:, 0:nq], in0=o2[:, 0:nq], in1=r2[:, 0:nq], op=MUL), reads=[o2k, r2k], writes=[r2k])
                        for _, k_ in accs:
                            psr.release(k_)
                        yp, ypk = tmpf.get()
                        Sc.op('dve', I('scalar_tensor_tensor', out=yp[:, 0:nq], in0=r2[:, 0:nq], scalar=LAM[:, l, 1:2], in1=r1[:, 0:nq], op0=MUL, op1=ADD),
                              reads=[r1k, r2k, ('LAM', l)], writes=[ypk])
                        Sc.op('act', I('activation', out=r1[:, 0:nq], in_=yp[:, 0:nq], func=AF.Square), reads=[ypk], writes=[r1k])
                        p2, p2k = psr.get()
                        Sc.op('pe', I('matmul', p2[:, 0:nq], lhsT=ONEF[:], rhs=r1[:, 0:nq], start=True, stop=True), reads=[r1k, 'ONEF'], writes=[p2k])
                        Sc.op('act', I('activation', out=r2[:, 0:nq], in_=p2[:, 0:nq], func=AF.Sqrt, scale=1.0 / 128, bias=EPS), reads=[p2k], writes=[r2k])
                        Sc.op('dve', I('reciprocal', out=r2[:, 0:nq], in_=r2[:, 0:nq]), reads=[r2k], writes=[r2k])
                        st, stk = tmpb.get()
                        Sc.op('dve', I('scalar_tensor_tensor', out=st[:, 0:nq], in0=yp[:, 0:nq], scalar=LAM[:, l, 2:3], in1=r2[:, 0:nq], op0=MUL, op1=MUL),
                              reads=[ypk, r2k, ('LAM', l)], writes=[stk])
                        for _, k_ in eacc:
                            tmpf.release(k_)
                        Sc.dma('sp', ('yst', stk[1]), I('dma_start', out=ysc_d[0, :, h, q0:q0 + nq], in_=st[:, 0:nq]),
                               reads=[stk], writes=[('ysc', 0, h, qg)])
                dump("ya%d" % l, ysc_d[0], [128, 4, TT], BF16, [('ysc', 0, h, g) for h in range(4) for g in range(ngrp)])
                if stop_after == "ya":
                    done()
                    return nc, list(dbg_out.keys())

                Sc.fence()
                for j in range(4):
                    s = j % 2
                    QC, KC, V2 = QKV[:, s, 0, :], QKV[:, s, 1, :], QKV[:, s, 2, :]
                    V2v = V2.rearrange("p (r c) -> p r c", c=64)
                    wq, wqk = wload(win_cols(l, OFF_CQ + j * 128, 128), 8, 128)
                    wk_, wkk = wload(win_cols(l, OFF_CK + j * 128, 128), 8, 128)
                    wv_, wvk = wload(win_cols(l, OFF_CV + j * 128, 128), 8, 128)
                    for pc in range(4):
                        tf, tfk = tmpf.get()
                        Sc.dma('sp', ('tabld', tfk[1]), I('dma_start', out=tf[:, 0:400], in_=rpbtab_d[l, j, :, pc * 400:(pc + 1) * 400]), writes=[tfk])
                        Sc.op('act', I('activation', out=tf[:, 0:400], in_=tf[:, 0:400], func=AF.Exp), reads=[tfk], writes=[tfk])
                        Sc.op('dve', I('tensor_tensor', out=TAB[:, pc * 400:(pc + 1) * 400], in0=tf[:, 0:400], in1=MASK[:, pc * 400:(pc + 1) * 400], op=MUL),
                              reads=[tfk, 'MASK'], writes=['TAB'])
                    for g in range(5):
                        t0, n = GROUPS[g]
                        nsub = n // 128
                        gens = []
                        psK, psKk = proj_fm(wk_, wkk, g)
                        gens.append(qknorm_gen(psK, psKk, n, SMALLP[:, l, 67:68], False, KC[:, t0:t0 + n], [('K', s, g)]))
                        if g < ngrp:
                            psQ, psQk = proj_fm(wq, wqk, g)
                            gens.append(qknorm_gen(psQ, psQk, n, SMALLP[:, l, 66:67], False, QC[:, t0:t0 + n], [('Q', s, g)]))
                        ps, psk = psr.get()
                        fns = []
                        for sub in range(nsub):
                            for k in range(8):
                                fns.append(I('matmul', ps[:, sub * 128:(sub + 1) * 128], lhsT=HT[:, k, t0 + sub * 128: t0 + (sub + 1) * 128],
                                             rhs=wv_[:, k, :], start=(k == 0), stop=(k == 7)))
                        Sc.op('pe', fns, reads=[wvk] + ht_keys(g), writes=[psk])
                        lockstep(gens)
                        vt, vtk = tmpb.get()
                        Sc.op('act', I('activation', out=vt[:, 0:n], in_=ps[:, 0:n], func=AF.Copy), reads=[psk], writes=[vtk])
                        ps2, ps2k = psr.get()
                        Sc.op('pe', [I('matmul', ps2[:, sub * 128:(sub + 1) * 128], lhsT=SWAPB[:], rhs=vt[:, sub * 128:(sub + 1) * 128], start=True, stop=True)
                                     for sub in range(nsub)], reads=[vtk, 'SWAPB'], writes=[ps2k])
                        r0 = t0 // 64
                        psv = ps[:, 0:n].rearrange("p (s c) -> p s c", c=128)
                        ps2v = ps2[:, 0:n].rearrange("p (s c) -> p s c", c=128)
                        V2g = V2v[:, r0:r0 + 2 * nsub, :].rearrange("p (s two) c -> p s two c", two=2)
                        Sc.op('dve', I('tensor_copy', out=V2g[0:64, :, 0, :], in_=psv[0:64, :, 0:64]), reads=[psk], writes=[('V', s, g, 0)])
                        Sc.op('act', I('activation', out=V2g[64:128, :, 1, :], in_=psv[64:128, :, 64:128], func=AF.Copy), reads=[psk], writes=[('V', s, g, 1)])
                        Sc.op('dve', I('tensor_copy', out=V2g[64:128, :, 0, :], in_=ps2v[64:128, :, 64:128]), reads=[ps2k], writes=[('V', s, g, 2)])
                        Sc.op('act', I('activation', out=V2g[0:64, :, 1, :], in_=ps2v[0:64, :, 0:64], func=AF.Copy), reads=[ps2k], writes=[('V', s, g, 3)])
                    if j == 0:
                        dump("cq%d" % l, QC, [128, TT], BF16, [('Q', s, g) for g in range(ngrp)])
                        dump("ck%d" % l, KC, [128, TT], BF16, [('K', s, g) for g in range(5)])
                        dump("cv%d" % l, V2, [128, TT], BF16, [('V', s, g, q) for g in range(5) for q in range(4)])
                    blocks = [(r, 1, list(range(0, 8))) for r in range(4)]
                    blocks += [(r, 2, list(range(r - 4, r + 5))) for r in range(4, 28, 2)]
                    blocks += [(r, 1, list(range(24, 32))) for r in range(28, 32)]
                    if not last:
                        blocks += [(32, 4, [])]
                    stage = None
                    for (r0, nr, krows) in blocks:
                        nq = 64 * nr
                        q0 = r0 * 64
                        qg = min(r0 // 8, 4)
                        if r0 % 8 == 0:
                            stage, stagek = tmpb.get()
                            tmpb.hold(stagek)
                        O_, Ok_ = psr.get()
                        psr.hold(Ok_)
                        Sm, Smk = psr.get()
                        psr.hold(Smk)
                        bsz = max(1, (512 // nq) // _CDIV)
                        loc = list(reversed(krows))
                        batches = [(loc[i:i + bsz], True) for i in range(0, len(loc), bsz)]
                        batches += [([32, 33, 34, 35][i:i + bsz], False) for i in range(0, 4, bsz)]
                        nsteps = sum(len(bt[0]) for bt in batches)

                        def emit_s(rows):
                            pS, pSk = psr.get()
                            fns = []
                            for i_, kr in enumerate(rows):
                                fns.append(I('matmul', pS[0:64, i_ * nq:(i_ + 1) * nq], lhsT=KC[0:64, kr * 64:(kr + 1) * 64], rhs=QC[0:64, q0:q0 + nq], start=True, stop=True))
                                fns.append(I('matmul', pS[64:128, i_ * nq:(i_ + 1) * nq], lhsT=KC[64:128, kr * 64:(kr + 1) * 64], rhs=QC[64:128, q0:q0 + nq], start=True, stop=True))
                            Sc.op('pe', fns, reads=list({('K', s, kr // 8) for kr in rows}) + [('Q', s, qg)], writes=[pSk])
                            return pS, pSk
                        LA = 2
                        pend = [emit_s(batches[i_][0]) for i_ in range(min(LA, len(batches)))]
                        sdone = 0
                        esum, esumk = tmpf.get()
                        tmpf.hold(esumk)
                        w0 = 0
                        for bi, (rows, is_loc) in enumerate(batches):
                            pS, pSk = pend.pop(0)
                            if bi + LA < len(batches):
                                pend.append(emit_s(batches[bi + LA][0]))
                            nb_ = len(rows)
                            ev, evk = tmpb.get()
                            Sc.op('act', I('activation', out=ev[:, 0:nb_ * nq], in_=pS[:, 0:nb_ * nq], func=AF.Exp, scale=0.125), reads=[pSk], writes=[evk])
                            if is_loc:
                                dr = rows[0] - r0
                                idx = (4 - dr) if nr == 2 else (17 - dr)
                                if nb_ == 1:
                                    Sc.op('dve', I('tensor_tensor', out=ev[:, 0:nq], in0=ev[:, 0:nq], in1=TAB[:, idx * 64: idx * 64 + nq], op=MUL),
                                          reads=[evk, 'TAB'], writes=[evk])
                                else:
                                    tv = TAB[:, idx * 64: idx * 64 + 64]
                                    win = bass.AP(tv.tensor, tv.offset, [list(tv.ap[0]), [64, nb_], [1, nq]])
                                    Sc.op('dve', I('tensor_tensor', out=ev[:, 0:nb_ * nq].rearrange("p (s c) -> p s c", c=nq),
                                                   in0=ev[:, 0:nb_ * nq].rearrange("p (s c) -> p s c", c=nq), in1=win, op=MUL),
                                          reads=[evk, 'TAB'], writes=[evk])
                            fns = []
                            for i_, kr in enumerate(rows):
                                first = (sdone == 0)
                                lastk = (sdone == nsteps - 1)
                                sdone += 1
                                fns += [I('matmul', O_[0:64, 0:nq], lhsT=V2v[0:64, kr, :], rhs=ev[0:64, i_ * nq:(i_ + 1) * nq], start=first, stop=lastk),
                                        I('matmul', O_[64:128, 0:nq], lhsT=V2v[64:128, kr, :], rhs=ev[64:128, i_ * nq:(i_ + 1) * nq], start=first, stop=lastk)]
                            Sc.op('pe', fns, reads=[evk] + [('V', s, g_, q) for g_ in {kr // 8 for kr in rows} for q in range(4)], writes=[Ok_])
                            wdt = nb_ * nq
                            if bi == 0:
                                Sc.op('pool', I('tensor_copy', out=esum[:, 0:wdt], in_=ev[:, 0:wdt]), reads=[evk], writes=[esumk])
                                w0 = wdt
                            else:
                                Sc.op('pool', I('tensor_tensor', out=esum[:, 0:wdt], in0=esum[:, 0:wdt], in1=ev[:, 0:wdt], op=ADD), reads=[evk, esumk], writes=[esumk])
                        nsl = w0 // nq
                        Sc.op('pe', [I('matmul', Sm[:, 0:nq], lhsT=bd64_f, rhs=esum[:, i_ * nq:(i_ + 1) * nq], start=(i_ == 0), stop=(i_ == nsl - 1)) for i_ in range(nsl)],
                              reads=[esumk, 'CONS'], writes=[Smk])
                        tmpf.release(esumk)
                        rr, rrk = tmpf.get()
                        if _CFIN:
                            Sc.op('dve', I('reciprocal', out=rr[:, 0:nq], in_=Sm[:, 0:nq]), reads=[Smk], writes=[rrk])
                        else:
                            Sc.op('act', I('activation', out=rr[:, 0:nq], in_=Sm[:, 0:nq], func=AF.Ln), reads=[Smk], writes=[rrk])
                            Sc.op('act', I('activation', out=rr[:, 0:nq], in_=rr[:, 0:nq], func=AF.Exp, scale=-1.0), reads=[rrk], writes=[rrk])
                        so = (r0 % 8) * 64
                        Sc.op('dve', I('tensor_tensor', out=stage[:, so:so + nq], in0=O_[:, 0:nq], in1=rr[:, 0:nq], op=MUL), reads=[Ok_, rrk], writes=[stagek])
                        psr.release(Ok_)
                        psr.release(Smk)
                        if (r0 + nr) % 8 == 0 or r0 == 32:
                            gq0, gn = GROUPS[qg]
                            Sc.dma('sp', ('yst', stagek[1]), I('dma_start', out=ysc_d[1, :, j, gq0:gq0 + gn], in_=stage[:, 0:gn]),
                                   reads=[stagek], writes=[('ysc', 1, j, qg)])
                            tmpb.release(stagek)
                dump("yc%d" % l, ysc_d[1], [128, 4, TT], BF16, [('ysc', 1, h, g) for h in range(4) for g in range(ngrp)])
                if stop_after == "yc":
                    Sc.fence()
                    done()
                    return nc, list(dbg_out.keys())

                Sc.fence()
                wring.held.discard(5)
                Sc.dma('sp', 'c4', I('dma_start', out=VNG, in_=vng_d[l].partition_broadcast(128)), writes=['VNG'])
                Sc.dma('sp', 'c5', I('dma_start', out=DBS, in_=dbs_d[l].partition_broadcast(128)), writes=['DBS'])
                Sc.dma('pool', 'c7', I('dma_start', out=WPOOL[:], in_=pool_w_d[l].rearrange("g c d -> c g d")), writes=['WPOOL'])
                Sc.dma('pool', 'c8', I('dma_start', out=WST[:], in_=wsT_d[l].rearrange("g q p -> q g p")), writes=['WST'])
                gate_bc(l, 16, 0, b)
                if not last:
                    gate_bc(l, 16, 1, nb)
                for g in range(ngrp):
                    t0, n = GROUPS[g]
                    nsub = n // 128
                    isctx = (g == 4)
                    slot = 1 if isctx else 0
                    seq0, seq1 = (2048, 2304) if isctx else (0, 2048)
                    has_l = t0 > seq0
                    has_r = t0 + n < seq1
                    Sc.dma('sp', 'yt0', I('dma_start', out=YT[:, 0, :, 0:n], in_=ysc_d[0, :, :, t0:t0 + n]),
                           reads=[('ysc', 0, h, g) for h in range(4)], writes=[('YT', 0)])
                    Sc.dma('sp', 'yt1', I('dma_start', out=YT[:, 1, :, 0:n], in_=ysc_d[1, :, :, t0:t0 + n]),
                           reads=[('ysc', 1, h, g) for h in range(4)], writes=[('YT', 1)])
                    for pg in range(4):
                        w = POOL_W[pg]
                        wB, wBk = wload_t(l, 40 + pg, 8, 128)
                        ps, psk = proj_fm(wB, wBk, g)
                        ph, phk = None, None
                        if has_l or has_r:
                            ph, phk = psr.get()
                            fns = []
                            rd = [wBk]
                            if has_l:
                                fns += [I('matmul', ph[:, 0:8], lhsT=wB[:, k, :], rhs=HT[:, k, t0 - 8:t0], start=(k == 0), stop=(k == 7)) for k in range(8)]
                                rd += ht_keys(g - 1)
                            if has_r:
                                fns += [I('matmul', ph[:, 8:16], lhsT=wB[:, k, :], rhs=HT[:, k, t0 + n:t0 + n + 8], start=(k == 0), stop=(k == 7)) for k in range(8)]
                                rd += ht_keys(g + 1)
                            Sc.op('pe', fns, reads=rd, writes=[phk])
                        pl, plk = tmpb.get()
                        for half in range(n // 256):
                            c0 = half * 256
                            pp, ppk = tmpf.get()
                            Sc.op('act', I('activation', out=pp[:, 8:264], in_=ps[:, c0:c0 + 256], func=AF.Copy), reads=[psk], writes=[ppk])
                            if half > 0:
                                Sc.op('dve', I('tensor_copy', out=pp[:, 0:8], in_=ps[:, c0 - 8:c0]), reads=[psk, ppk], writes=[ppk])
                            elif has_l:
                                Sc.op('dve', I('tensor_copy', out=pp[:, 0:8], in_=ph[:, 0:8]), reads=[phk, ppk], writes=[ppk])
                            else:
                                Sc.op('dve', I('memset', pp[:, 0:8], 0.0), reads=[ppk], writes=[ppk])
                            if c0 + 256 < n:
                                Sc.op('dve', I('tensor_copy', out=pp[:, 264:272], in_=ps[:, c0 + 256:c0 + 264]), reads=[psk, ppk], writes=[ppk])
                            elif has_r:
                                Sc.op('dve', I('tensor_copy', out=pp[:, 264:272], in_=ph[:, 8:16]), reads=[phk, ppk], writes=[ppk])
                            else:
                                Sc.op('dve', I('memset', pp[:, 264:272], 0.0), reads=[ppk], writes=[ppk])
                            b1, b1k = tmpf.get()
                            Sc.op('dve', I('tensor_tensor', out=b1[:, 1:272], in0=pp[:, 0:271], in1=pp[:, 1:272], op=ADD), reads=[ppk], writes=[b1k])
                            cur, curk = b1, b1k
                            if w >= 4:
                                b2, b2k = tmpf.get()
                                Sc.op('dve', I('tensor_tensor', out=b2[:, 2:271], in0=b1[:, 1:270], in1=b1[:, 3:272], op=ADD), reads=[b1k], writes=[b2k])
                                cur, curk = b2, b2k
                            if w >= 8:
                                Sc.op('dve', I('tensor_tensor', out=b1[:, 4:269], in0=b2[:, 2:267], in1=b2[:, 6:271], op=ADD), reads=[b2k], writes=[b1k])
                                cur, curk = b1, b1k
                            if w >= 16:
                                Sc.op('dve', I('tensor_tensor', out=b2[:, 8:264], in0=b1[:, 4:260], in1=b1[:, 12:268], op=ADD), reads=[b1k], writes=[b2k])
                                cur, curk = b2, b2k
                            if half == 0 and not has_l:
                                cc = C_CORR + pg * 16
                                Sc.op('dve', I('tensor_tensor', out=cur[:, 8:16], in0=cur[:, 8:16], in1=CONS[:, cc:cc + 8], op=MUL), reads=[curk, 'CONS'], writes=[curk])
                            if c0 + 256 == n and not has_r:
                                cc = C_CORR + pg * 16 + 8
                                Sc.op('dve', I('tensor_tensor', out=cur[:, 256:264], in0=cur[:, 256:264], in1=CONS[:, cc:cc + 8], op=MUL), reads=[curk, 'CONS'], writes=[curk])
                            Sc.op('dve', I('scalar_tensor_tensor', out=pl[:, c0:c0 + 256], in0=cur[:, 8:264], scalar=1.0 / w, in1=pp[:, 8:264], op0=MUL, op1=SUB),
                                  reads=[curk, ppk], writes=[plk])
                        psY, psYk = psr.get()
                        Sc.op('pe', I('matmul', psY[:, 0:n], lhsT=WPOOL[:, pg, :], rhs=pl[:, 0:n], start=True, stop=True), reads=[plk, 'WPOOL'], writes=[psYk])
                        Sc.op('act', I('activation', out=YBD[:, 0, pg, 0:n], in_=psY[:, 0:n], func=AF.Copy, scale=SMALLP[:, l, 69 + pg:70 + pg]),
                              reads=[psYk, 'SMALLP'], writes=[('YB', pg)])
                    wdv = [wload_t(l, 48 + hh, 8, 256) for hh in range(2)]
                    Sc.op('dve', I('memset', SS1[:, 4:8], 0.0), writes=[('SS1d', sub) for sub in range(4)])
                    for sub in range(nsub):
                        ps, psk = psr.get()
                        fns = []
                        for hh in range(2):
                            for k in range(8):
                                fns.append(I('matmul', ps[:, hh * 256:(hh + 1) * 256], lhsT=HT[:, k, t0 + sub * 128:t0 + (sub + 1) * 128],
                                             rhs=wdv[hh][0][:, k, :], start=(k == 0), stop=(k == 7)))
                        Sc.op('pe', fns, reads=[wdv[0][1], wdv[1][1], ('HT', g, sub), ('HTb', g, sub)], writes=[psk])
                        jk, jkk = tmpb.get()
                        Sc.op('act', I('activation', out=jk[:, 0:512], in_=ps[:, 0:512], func=AF.Square, accum_out=SS1[:, 4 + sub:5 + sub]),
                              reads=[psk], writes=[jkk, ('SS1d', sub)])
                        Sc.op('act', I('activation', out=SS1[:, 4 + sub:5 + sub], in_=SS1[:, 4 + sub:5 + sub], func=AF.Sqrt, scale=1.0 / 512, bias=EPS), reads=[('SS1d', sub)], writes=[('SS1d', sub)])
                        Sc.op('dve', I('reciprocal', out=SS1[:, 4 + sub:5 + sub], in_=SS1[:, 4 + sub:5 + sub]), reads=[('SS1d', sub)], writes=[('SS1d', sub)])
                        Sc.op('dve', I('scalar_tensor_tensor', out=VN[:, sub, :], in0=ps[:, 0:512], scalar=SS1[:, 4 + sub:5 + sub], in1=VNG, op0=MUL, op1=MUL),
                              reads=[psk, ('SS1d', sub), 'VNG'], writes=[('VN', sub)])
                    for pg in range(4):
                        wu, wuk = wload_t(l, 44 + pg, 8, 128)
                        psU, psUk = proj_fm(wu, wuk, g)
                        psr.hold(psUk)
                        psV, psVk = psr.get()
                        Sc.op('pe', [I('matmul', psV[:, sub * 128:(sub + 1) * 128], lhsT=VN[:, sub, pg * 128:(pg + 1) * 128], rhs=WST[:, pg, :], start=True, stop=True)
                                     for sub in range(nsub)], reads=[('VN', sub) for sub in range(nsub)] + ['WST'], writes=[psVk])
                        tt_, ttk = tmpf.get()
                        Sc.op('dve', I('tensor_tensor', out=tt_[:, 0:n].rearrange("p (s c) -> p s c", c=128), in0=psV[:, 0:n].rearrange("p (s c) -> p s c", c=128),
                                       in1=DBS[:, pg * 128:(pg + 1) * 128].unsqueeze(1).to_broadcast([128, nsub, 128]), op=ADD),
                              reads=[psVk, 'DBS'], writes=[ttk])
                        Sc.op('dve', I('tensor_tensor', out=YBD[:, 1, pg, 0:n], in0=psU[:, 0:n], in1=tt_[:, 0:n], op=MUL), reads=[psUk, ttk], writes=[('YD', pg)])
                        psr.release(psUk)
                    if g == 0:
                        dump("ybd%d" % l, YBD, [128, 2, 4, 512], BF16, [('YB', pg) for pg in range(4)] + [('YD', pg) for pg in range(4)])
                    ysrc = [YT[:, 0], YBD[:, 0], YT[:, 1], YBD[:, 1]]
                    ykeys = [[('YT', 0)], [('YB', pg) for pg in range(4)], [('YT', 1)], [('YD', pg) for pg in range(4)]]
                    for j in range(8):
                        wb, wbk = wload_t(l, 32 + j, 16, 128)
                        wgs = []
                        for ip in range(2):
                            v_, key_ = wring.get()
                            for hh in range(2):
                                i_ = ip * 2 + hh
                                view = v_[:, hh * 1024:(hh + 1) * 1024].rearrange("p (k c) -> p k c", k=8)
                                src = wsc_d[l, j * 4 + i_, :, 0:1024].rearrange("p (k c) -> p k c", k=8)
                                if hh == 0:
                                    Sc.dma('sp', key_, I('dma_start', out=view, in_=src), writes=[key_, (key_, 1)])
                                    wgs.append((view, key_))
                                else:
                                    Sc.dma('sp', (key_, 'b'), I('dma_start', out=view, in_=src), writes=[(key_, 1)])
                                    wgs.append((view, (key_, 1)))
                        acc, acck = tmpf.get()
                        tmpf.hold(acck)
                        for i in range(4):
                            psZ, psZk = psr.get()
                            Sc.op('pe', [I('matmul', psZ[:, 0:n], lhsT=wb[:, i * 4 + k, :], rhs=ysrc[i][:, k, 0:n], start=(k == 0), stop=(k == 3)) for k in range(4)],
                                  reads=[wbk] + ykeys[i], writes=[psZk])
                            psG, psGk = psr.get()
                            wg, wgk = wgs[i]
                            Sc.op('pe', [I('matmul', psG[:, 0:n], lhsT=wg[:, k, :], rhs=HT[:, k, t0:t0 + n], start=(k == 0), stop=(k == 7)) for k in range(8)],
                                  reads=[wgk] + ht_keys(g), writes=[psGk])
                            sg, sgk = tmpf.get()
                            Sc.op('act', I('activation', out=sg[:, 0:n], in_=psG[:, 0:n], func=AF.Sigmoid), reads=[psGk], writes=[sgk])
                            if i == 0:
                                Sc.op('dve', I('tensor_tensor', out=acc[:, 0:n], in0=psZ[:, 0:n], in1=sg[:, 0:n], op=MUL), reads=[psZk, sgk], writes=[acck])
                            else:
                                Sc.op('dve', I('tensor_tensor', out=sg[:, 0:n], in0=psZ[:, 0:n], in1=sg[:, 0:n], op=MUL), reads=[psZk, sgk], writes=[sgk])
                                if i < 3:
                                    Sc.op('dve', I('tensor_tensor', out=acc[:, 0:n], in0=acc[:, 0:n], in1=sg[:, 0:n], op=ADD), reads=[acck, sgk], writes=[acck])
                                else:
                                    Sc.op('dve', I('tensor_tensor', out=ACCT[:, j, 0:n], in0=acc[:, 0:n], in1=sg[:, 0:n], op=ADD), reads=[acck, sgk], writes=[('ACCT', j)])
                        tmpf.release(acck)
                    for qd in range(4):
                        wo, wok = wload_t(l, 50 + qd, 8, 256)
                        for sub in range(nsub):
                            tt = t0 // 128 + sub
                            psO, psOk = psr.get()
                            Sc.op('pe', [I('matmul', psO[:, 0:256], lhsT=ACCT[:, k, sub * 128:(sub + 1) * 128], rhs=wo[:, k, :], start=(k == 0), stop=(k == 7)) for k in range(8)],
                                  reads=[wok] + [('ACCT', k) for k in range(8)], writes=[psOk])
                            tq, tqk = tmpf.get()
                            Sc.op('dve', I('tensor_tensor', out=tq[:, 0:256], in0=psO[:, 0:256], in1=GBC[:, slot, qd * 256:(qd + 1) * 256], op=MUL),
                                  reads=[psOk, ('GBC', slot)], writes=[tqk])
                            Sc.op('dve', I('tensor_tensor', out=X[:, tt, qd * 256:(qd + 1) * 256], in0=X[:, tt, qd * 256:(qd + 1) * 256], in1=tq[:, 0:256], op=ADD),
                                  reads=[tqk, ('X', tt)], writes=[('X', tt)])
                dump("xa%d" % l, X[:], [128, NTT, D], F32, [('X', tt) for tt in range(NTT)])
                if stop_after == "xa":
                    Sc.fence()
                    done()
                    return nc, list(dbg_out.keys())

                Sc.fence()
                ftts = list(range(16)) if last else list(range(NTT))
                norm_stats(1)
                make_ht(1, 4, ftts)
                gate_bc(l, 40, 0, b)
                if not last:
                    gate_bc(l, 40, 1, nb)
                ACT2, _ = carve(8192, [128, 2, 2, 512], BF16)
                WDS, _ = carve(12288, [128, 2, 2, 1024], BF16)
                for hg in range(FFN // 256):
                    wa, wak = wload(w_gu_d[l, :, hg * 256:(hg + 1) * 256].rearrange("(k p) n -> p k n", p=128), 8, 256)
                    wbb, wbbk = wload(w_gu_d[l, :, FFN + hg * 256:FFN + (hg + 1) * 256].rearrange("(k p) n -> p k n", p=128), 8, 256)
                    wd, wdk = wload(w_down_d[l, hg * 256:(hg + 1) * 256, :].rearrange("(k p) n -> p k n", p=128), 2, 1024)
                    wsb = hg % 2
                    Sc.op('dve', I('tensor_tensor', out=WDS[:, wsb], in0=wd, in1=GBC[:, 0, :].unsqueeze(1).to_broadcast([128, 2, D]), op=MUL),
                          reads=[wdk, ('GBC', 0)], writes=[('WDS', wsb)])
                    for g in range(ngrp):
                        t0, n = GROUPS[g]
                        nsub = n // 128
                        slot = 1 if g == 4 else 0
                        ab = (hg * ngrp + g) % 2
                        for c2 in range(2):
                            psA, psAk = proj_fm(wa, wak, g, col0=c2 * 128)
                            psB, psBk = proj_fm(wbb, wbbk, g, col0=c2 * 128)
                            sa, sak = tmpf.get()
                            Sc.op('act', I('activation', out=sa[:, 0:n], in_=psA[:, 0:n], func=AF.Silu), reads=[psAk], writes=[sak])
                            Sc.op('dve', I('tensor_tensor', out=ACT2[:, ab, c2, 0:n], in0=psB[:, 0:n], in1=sa[:, 0:n], op=MUL), reads=[psBk, sak], writes=[('ACT2', ab, c2)])
                        for sub in range(nsub):
                            tt = t0 // 128 + sub
                            for half in range(2):
                                psO, psOk = psr.get()
                                if g < 4:
                                    Sc.op('pe', [I('matmul', psO[:, 0:512], lhsT=ACT2[:, ab, c2, sub * 128:(sub + 1) * 128], rhs=WDS[:, wsb, c2, half * 512:(half + 1) * 512],
                                                   start=(c2 == 0), stop=(c2 == 1)) for c2 in range(2)],
                                          reads=[('WDS', wsb), ('ACT2', ab, 0), ('ACT2', ab, 1)], writes=[psOk])
                                    Sc.op('dve', I('tensor_tensor', out=X[:, tt, half * 512:(half + 1) * 512], in0=psO[:, 0:512], in1=X[:, tt, half * 512:(half + 1) * 512], op=ADD),
                                          reads=[psOk, ('X', tt)], writes=[('X', tt)])
                                    continue
                                Sc.op('pe', [I('matmul', psO[:, 0:512], lhsT=ACT2[:, ab, c2, sub * 128:(sub + 1) * 128], rhs=wd[:, c2, half * 512:(half + 1) * 512],
                                               start=(c2 == 0), stop=(c2 == 1)) for c2 in range(2)],
                                      reads=[wdk, ('ACT2', ab, 0), ('ACT2', ab, 1)], writes=[psOk])
                                tq, tqk = tmpf.get()
                                Sc.op('dve', I('tensor_tensor', out=tq[:, 0:512], in0=psO[:, 0:512], in1=GBC[:, slot, half * 512:(half + 1) * 512], op=MUL),
                                      reads=[psOk, ('GBC', slot)], writes=[tqk])
                                Sc.op('dve', I('tensor_tensor', out=X[:, tt, half * 512:(half + 1) * 512], in0=X[:, tt, half * 512:(half + 1) * 512], in1=tq[:, 0:512], op=ADD),
                                      reads=[tqk, ('X', tt)], writes=[('X', tt)])
                wring.held.add(5)
                dump("xo%d" % l, X[:], [128, NTT, D], F32, [('X', tt) for tt in range(NTT)])
                if stop_after == "xo%d" % l:
                    Sc.fence()
                    done()
                    return nc, list(dbg_out.keys())
                Sc.fence()
            for q in range(4):
                Sc.dma('sp', 'yout%d' % q, I('dma_start', out=y_d[b * S + q * 512: b * S + (q + 1) * 512, :].rearrange("(t p) d -> p t d", p=128),
                                             in_=X[:, q * 4:(q + 1) * 4, :]), reads=[('X', q * 4 + i) for i in range(4)])
        done()
    return nc, list(dbg_out.keys())


def _const_tables():
    c = np.zeros((128, C_END), np.float32)
    c[:, C_IDENT:C_IDENT + 128] = np.eye(128, dtype=np.float32)
    bd = np.zeros((128, 128), np.float32)
    bd[0:64, 0:64] = 1.0
    bd[64:128, 64:128] = 1.0
    c[:, C_BD64:C_BD64 + 128] = bd
    rm = np.zeros((128, 128), np.float32)
    for blk in (0, 64):
        for q in range(4):
            for i in range(16):
                o = blk + q * 16 + i
                if q % 2 == 0:
                    rm[o + 16, o] = -1.0
                else:
                    rm[o - 16, o] = 1.0
    c[:, C_RM:C_RM + 128] = rm
    nf = 16
    inv = (10000.0 ** (-np.arange(nf, dtype=np.float32) / nf)).astype(np.float32)
    rows = np.arange(32, dtype=np.float32)
    cols = np.arange(64, dtype=np.float32)
    for cs, fn in enumerate((np.cos, np.sin)):
        base = C_ROPE + cs * 96
        for p in range(128):
            d = p % 64
            f = d % 16
            if d < 32:
                c[p, base:base + 32] = fn((rows * inv[f]).astype(np.float32))
            else:
                c[p, base + 32:base + 96] = fn((cols * inv[f]).astype(np.float32))
    L = 1 << 20
    for g, w in enumerate(POOL_W):
        for t in range(8):
            lo, hi = max(t - w // 2, 0), t + w // 2
            c[:, C_CORR + g * 16 + t] = w / float(hi - lo)
            te = L - 8 + t
            lo, hi = te - w // 2, min(te + w // 2, L)
            c[:, C_CORR + g * 16 + 8 + t] = w / float(hi - lo)
    sw = np.zeros((128, 128), np.float32)
    for i in range(128):
        sw[(i + 64) % 128, i] = 1.0
    c[:, C_SWAP:C_SWAP + 128] = sw
    return c


def _rpb_layout(c_rpb):
    Ld = c_rpb.shape[0]
    kc = np.arange(64)[:, None]
    qc = np.arange(64)[None, :]
    dc = np.clip(kc - qc + 15, 0, 30)
    win0 = np.clip(qc - 8, 0, 48)
    colvalid = ((kc >= win0) & (kc < win0 + 16)).astype(np.float32)
    drs = [4, 3, 2, 1, 0, -1, -2, -3, -4, -5] + list(range(7, -8, -1))
    rowvalid = [0, 1, 1, 1, 1, 1, 1, 1, 1, 0] + [1] * 15
    tab = np.zeros((Ld, 4, 128, 25, 64), np.float32)
    mask = np.zeros((128, 25, 64), np.float32)
    for t, (dr, rv) in enumerate(zip(drs, rowvalid)):
        if not rv:
            continue
        for hh in range(2):
            mask[hh * 64:(hh + 1) * 64, t, :] = colvalid
            for j in range(4):
                tab[:, j, hh * 64:(hh + 1) * 64, t, :] = c_rpb[:, 2 * j + hh, dr + 7][:, dc]
    return tab.reshape(Ld, 4, 128, NTAB), mask.reshape(128, NTAB)


def prep_shared(inp):
    f = lambda a: np.ascontiguousarray(np.asarray(a, dtype=np.float32))
    Ld = DEPTH
    sp = np.zeros((Ld, 128, NSP), np.float32)
    sp[:, :, 0:48] = f(inp['b_mod']).reshape(Ld, 48, 128).transpose(0, 2, 1)
    sp[:, :, 48:56] = f(inp['norm1_g']).reshape(Ld, 8, 128).transpose(0, 2, 1)
    sp[:, :, 56:64] = f(inp['norm2_g']).reshape(Ld, 8, 128).transpose(0, 2, 1)
    p64 = np.arange(128) % 64
    sp[:, :, 64] = f(inp['a_qk_g'])[:, 0][:, p64]
    sp[:, :, 65] = f(inp['a_qk_g'])[:, 1][:, p64]
    sp[:, :, 66] = f(inp['c_qk_g'])[:, 0][:, p64]
    sp[:, :, 67] = f(inp['c_qk_g'])[:, 1][:, p64]
    sp[:, :, 68] = f(inp['a_subln_g'])
    sp[:, :, 69:73] = f(inp['b_pool_s']).reshape(Ld, 4, 128).transpose(0, 2, 1)
    tab, mask = _rpb_layout(f(inp['c_rpb']))
    return {
        "w_mod": f(inp['w_mod']), "w_in": f(inp['w_in']), "w_branch": f(inp['w_branch']), "w_out": f(inp['w_out']),
        "w_gu": f(inp['w_gu']), "w_down": f(inp['w_down']), "b_pool_w": f(inp['b_pool_w']),
        "d_wsT": np.ascontiguousarray(f(inp['d_ws']).transpose(0, 1, 3, 2)),
        "smallp": sp, "vn_g": f(inp['d_vn_g']), "d_bs": f(inp['d_bs']).reshape(Ld, 512),
        "a_lambda": f(inp['a_lambda']).reshape(Ld, 256), "rpbtab": tab, "rpbmask": mask, "consts": _const_tables(),
    }


def prep_core(inp, shared, b0, nb):
    f = lambda a: np.ascontiguousarray(np.asarray(a, dtype=np.float32))
    m = dict(shared)
    m["x"] = f(inp['x'][b0:b0 + nb]).reshape(nb * S, D)
    m["ctx"] = f(inp['ctx'][b0:b0 + nb]).reshape(nb * CTX, D)
    cs = np.concatenate([f(inp['c'][b0:b0 + nb]), f(inp['c_ctx'])[None, :]], axis=0)
    m["cT"] = np.ascontiguousarray(cs.reshape(nb + 1, 8, 128).transpose(2, 1, 0)).reshape(128, 8 * (nb + 1))
    return m


_CACHE = {}


def kernel(**inputs):
    nb = inputs['x'].shape[0] // N_CORES
    if 'nc' not in _CACHE:
        _CACHE['nc'] = build(nb=nb)[0]
    nc = _CACHE['nc']
    shared = prep_shared(inputs)
    in_maps = [prep_core(inputs, shared, c * nb, nb) for c in range(N_CORES)]
    res = run_bass_kernel_spmd(nc, in_maps, core_ids=list(range(N_CORES)))
    out = np.concatenate([np.asarray(r["y"]).reshape(nb, S, D) for r in res.results], axis=0)
    return out.astype(np.float32)
```
